# Optimizing a Trainium2 kernel written in Bass

```python
import math
import jax, jax.numpy as jnp
from jax import lax
import numpy as np

D_MODEL = 1024
BATCH = 16
SEQ = 2048
DEPTH = 1

HEAD_DIM = 64
SB_HEADS = 8
DIL_PAIRS = ((128, 1), (512, 4), (2048, 16))
DIL_HEADS_PER_GROUP = 4
DIL_HEADS = DIL_HEADS_PER_GROUP * len(DIL_PAIRS)
SB_WIDTH = SB_HEADS * HEAD_DIM
DIL_WIDTH = DIL_HEADS * HEAD_DIM
DIL_OUT_WIDTH = DIL_HEADS_PER_GROUP * HEAD_DIM
IN_WIDTH = 3 * SB_WIDTH + 3 * DIL_WIDTH + 2 * D_MODEL
D_FF = ((8 * D_MODEL + 3 * 256 - 1) // (3 * 256)) * 256
Q_BLOCK = 128
RMS_EPS = 1e-6
ALIBI_MAX_BIAS = 8.0
SPLITS = tuple(int(c) for c in np.cumsum([SB_WIDTH, SB_WIDTH, SB_WIDTH, DIL_WIDTH, DIL_WIDTH, DIL_WIDTH, D_MODEL]))

kernel_name = "hybrid_stickbreak_dilated_gated"


def rms_norm(x, g):
    xf = x.astype(jnp.float32)
    y = xf * lax.rsqrt(jnp.mean(xf * xf, axis=-1, keepdims=True) + RMS_EPS) * g.astype(jnp.float32)
    return y.astype(x.dtype)


def alibi_slopes(n):
    return jnp.exp2(-ALIBI_MAX_BIAS * jnp.arange(1, n + 1, dtype=jnp.float32) / n)


def stick_breaking_attention(q, k, v):
    b, s, h, dh = q.shape
    nb = s // Q_BLOCK
    scale = 1.0 / math.sqrt(dh)
    qb = q.reshape(b, nb, Q_BLOCK, h, dh).transpose(1, 0, 3, 2, 4)
    kpos = jnp.arange(s)

    def block(args):
        q_blk, t0 = args
        z = jnp.einsum('bhqd,bkhd->bhqk', q_blk, k, preferred_element_type=jnp.float32) * scale
        tpos = t0 + jnp.arange(Q_BLOCK)
        causal = kpos[None, :] < tpos[:, None]
        log_keep = jnp.where(causal, jax.nn.log_sigmoid(-z), 0.0)
        log_after = lax.cumsum(log_keep, axis=3, reverse=True) - log_keep
        a = jnp.where(causal, jnp.exp(jax.nn.log_sigmoid(z) + log_after), 0.0)
        return jnp.einsum('bhqk,bkhd->bqhd', a.astype(v.dtype), v)

    out = lax.map(block, (qb, jnp.arange(nb) * Q_BLOCK))
    return out.transpose(1, 0, 2, 3, 4).reshape(b, s, h * dh)


def dilated_group_attention(q, k, v, window, dilation, slopes):
    b, s, h, dh = q.shape
    L = s // dilation
    w = window // dilation
    blk = w
    nb = -(-L // blk)
    lp = nb * blk

    def to_sub(t):
        t = t.reshape(b, L, dilation, h, dh).transpose(0, 2, 3, 1, 4)
        return jnp.pad(t, ((0, 0), (0, 0), (0, 0), (0, lp - L), (0, 0)))

    def band(t):
        t = jnp.pad(t, ((0, 0), (0, 0), (0, 0), (blk, 0), (0, 0))).reshape(b, dilation, h, nb + 1, blk, dh)
        return jnp.concatenate([t[:, :, :, :-1], t[:, :, :, 1:]], axis=4)

    qb = to_sub(q).reshape(b, dilation, h, nb, blk, dh)
    kb = band(to_sub(k))
    vb = band(to_sub(v))
    scores = jnp.einsum('brhnqd,brhnkd->brhnqk', qb, kb, preferred_element_type=jnp.float32) / math.sqrt(dh)
    qa = jnp.arange(blk)
    kc = jnp.arange(2 * blk)
    dist = blk + qa[:, None] - kc[None, :]
    key_idx = (jnp.arange(nb)[:, None] - 1) * blk + kc[None, :]
    valid = ((dist >= 0) & (dist <= w))[None, :, :] & (key_idx >= 0)[:, None, :]
    scores = scores - slopes[:, None, None, None] * (dist * dilation).astype(jnp.float32)
    scores = jnp.where(valid, scores, -jnp.inf)
    m = scores.max(-1)
    p = jnp.exp(scores - m[..., None])
    l = p.sum(-1)
    num = jnp.einsum('brhnqk,brhnkd->brhnqd', p, vb.astype(jnp.float32))

    def from_sub(t):
        t = t.reshape((b, dilation, h, lp) + t.shape[5:])[:, :, :, :L]
        t = jnp.moveaxis(t, 3, 1)
        return t.reshape((b, s, h) + t.shape[4:])

    return from_sub(num), from_sub(m), from_sub(l)


def dilated_mixture_attention(q, k, v):
    b, s, _, dh = q.shape
    slopes = alibi_slopes(DIL_HEADS)
    nums, ms, ls = [], [], []
    for g, (window, dilation) in enumerate(DIL_PAIRS):
        sl = slice(g * DIL_HEADS_PER_GROUP, (g + 1) * DIL_HEADS_PER_GROUP)
        n_g, m_g, l_g = dilated_group_attention(q[:, :, sl], k[:, :, sl], v[:, :, sl], window, dilation, slopes[sl])
        nums.append(n_g); ms.append(m_g); ls.append(l_g)
    m = jnp.stack(ms)
    wts = jnp.exp(m - m.max(0))
    den = (wts * jnp.stack(ls)).sum(0)
    num = (wts[..., None] * jnp.stack(nums)).sum(0)
    out = num / den[..., None]
    return out.reshape(b, s, DIL_OUT_WIDTH)


def setup_inputs(seed: int = 0) -> dict:
    key = jax.random.key(seed)
    ks = jax.random.split(key, 11)
    f32 = jnp.float32

    def nrm(k, shape, fan_in):
        return jax.random.normal(k, shape, f32) * (fan_in ** -0.5)

    return {
        "x": jax.random.normal(ks[0], (BATCH, SEQ, D_MODEL), f32),
        "norm_mix_g": 1.0 + 0.01 * jax.random.normal(ks[1], (DEPTH, D_MODEL), f32),
        "w_in": nrm(ks[2], (DEPTH, D_MODEL, IN_WIDTH), D_MODEL),
        "w_sb_up": nrm(ks[3], (DEPTH, SB_WIDTH, D_MODEL), SB_WIDTH),
        "w_dil_up": nrm(ks[4], (DEPTH, DIL_OUT_WIDTH, D_MODEL), DIL_OUT_WIDTH),
        "w_out": nrm(ks[5], (DEPTH, D_MODEL, D_MODEL), D_MODEL),
        "norm_ffn_g": 1.0 + 0.01 * jax.random.normal(ks[6], (DEPTH, D_MODEL), f32),
        "w_ffn_in": nrm(ks[7], (DEPTH, D_MODEL, 2 * D_FF), D_MODEL),
        "w_ffn_out": nrm(ks[8], (DEPTH, D_FF, D_MODEL), D_FF),
        "norm_final_g": 1.0 + 0.01 * jax.random.normal(ks[9], (D_MODEL,), f32),
    }


def reference(x, norm_mix_g, w_in, w_sb_up, w_dil_up, w_out, norm_ffn_g, w_ffn_in, w_ffn_out, norm_final_g):
    b, s, _ = x.shape
    for i in range(DEPTH):
        u = rms_norm(x, norm_mix_g[i])
        proj = u @ w_in[i]
        q_sb, k_sb, v_sb, q_dl, k_dl, v_dl, gate_sb, gate_dl = jnp.split(proj, SPLITS, axis=-1)
        heads_sb = lambda t: t.reshape(b, s, SB_HEADS, HEAD_DIM)
        heads_dl = lambda t: t.reshape(b, s, DIL_HEADS, HEAD_DIM)
        o_sb = stick_breaking_attention(heads_sb(q_sb), heads_sb(k_sb), heads_sb(v_sb))
        o_dl = dilated_mixture_attention(heads_dl(q_dl), heads_dl(k_dl), heads_dl(v_dl)).astype(x.dtype)
        y_sb = o_sb @ w_sb_up[i]
        y_dl = o_dl @ w_dil_up[i]
        merged = jax.nn.sigmoid(gate_sb) * y_sb + jax.nn.sigmoid(gate_dl) * y_dl
        x = x + merged @ w_out[i]
        u2 = rms_norm(x, norm_ffn_g[i])
        g_ff, up_ff = jnp.split(u2 @ w_ffn_in[i], 2, axis=-1)
        x = x + (jax.nn.silu(g_ff) * up_ff) @ w_ffn_out[i]
    return rms_norm(x, norm_final_g)
```

```python
import bisect
import contextlib
import math

import numpy as np
import concourse.bass as bass
import concourse.mybir as mybir
from concourse.bass_utils import run_bass_kernel_spmd

F32 = mybir.dt.float32
BF16 = mybir.dt.bfloat16
AF = mybir.ActivationFunctionType
ALU = mybir.AluOpType

D = 1024
S = 2048
NSEQ = 2
NT = S // 128
DFF = 2816
NFF = DFF // 128
INW = 5888
EPS = 1e-6
NEG = -30000.0
PRECAST = True
NWF = 4
PFD = 3
DIL = (1, 4, 16)


class Eng:
    def __init__(self, nc, eng, name, es, is_pe=False):
        self.eng = eng
        self.name = name
        self.sem = es.enter_context(nc.semaphore("sem_" + name))
        self.n = 0
        self.last = None
        self.mark_idx = []
        self.mark_cnt = []
        self.count = 0
        self.seen = {}
        self.is_pe = is_pe
        self.log = []
        self.instrs = []
        self.last_entry = None

    def emit(self, ins):
        self.last = ins
        self.n += 1
        self.last_entry = ["instr", self.n - 1, None]
        self.log.append(self.last_entry)
        self.instrs.append((ins, self.last_entry))
        return (self, self.n - 1)

    def threshold(self, idx):
        i = bisect.bisect_left(self.mark_idx, idx)
        if i < len(self.mark_idx):
            return self.mark_cnt[i]
        ins, ent = self.instrs[idx]
        ins.then_inc(self.sem, 1)
        ent[2] = self.name
        self.count += 1
        self.mark_idx.append(idx)
        self.mark_cnt.append(self.count)
        return self.count

    def wait_token(self, tok):
        if tok is None:
            return
        if tok[0] == "dma":
            _, sem, val, key = tok
        else:
            prod, idx = tok
            if prod is self and self.is_pe:
                return
            val = prod.threshold(idx)
            sem = prod.sem
            key = prod.name
        if self.seen.get(key, 0) >= val:
            return
        self.eng.wait_ge(sem, val)
        self.log.append(["wait", key, val])
        self.seen[key] = val


class Buf:
    __slots__ = ("w", "r", "wx")

    def __init__(self):
        self.w = None
        self.r = []
        self.wx = []

    def add_reader(self, tok):
        if tok[0] != "dma":
            self.r = [t for t in self.r if t[0] == "dma" or t[0] is not tok[0]]
        self.r.append(tok)


class Ctx:
    def __init__(self, nc, es):
        self.nc = nc
        self.pe = Eng(nc, nc.tensor, "pe", es, is_pe=True)
        self.act = Eng(nc, nc.scalar, "act", es)
        self.dve = Eng(nc, nc.vector, "dve", es)
        self.sp = Eng(nc, nc.sync, "sp", es)
        self.pool = Eng(nc, nc.gpsimd, "pool", es)
        self.dsems = {}
        for q in ("sp", "pool"):
            self.dsems[q] = [[es.enter_context(nc.semaphore("dsem_%s_%d" % (q, i))), 0] for i in range(12)]
        self.drr = {"sp": 0, "pool": 0}
        self.bar_sp = es.enter_context(nc.semaphore("bar_sp"))
        self.bar_pool = es.enter_context(nc.semaphore("bar_pool"))
        self.nbar = 0

    def _deps(self, E, reads, writes):
        for b in reads:
            E.wait_token(b.w)
            for t in b.wx:
                E.wait_token(t)
        for b in writes:
            E.wait_token(b.w)
            for t in b.wx:
                E.wait_token(t)
            for r in b.r:
                E.wait_token(r)

    def op(self, E, ins_fn, reads=(), writes=()):
        self._deps(E, reads, writes)
        ins = ins_fn()
        tok = E.emit(ins)
        for b in reads:
            b.add_reader(tok)
        for b in writes:
            b.w = tok
            b.wx = []
            b.r = []
        return tok

    def dma(self, qname, out, in_, reads=(), writes=()):
        Q = self.sp if qname == "sp" else self.pool
        self._deps(Q, reads, writes)
        lst = self.dsems[qname]
        i = self.drr[qname]
        self.drr[qname] = (i + 1) % len(lst)
        sem, cnt = lst[i]
        key = "d_%s_%d" % (qname, i)
        if cnt > 0 and Q.seen.get(key, 0) < cnt:
            Q.eng.wait_ge(sem, cnt)
            Q.log.append(["wait", key, cnt])
            Q.seen[key] = cnt
        Q.eng.dma_start(out=out, in_=in_).then_inc(sem, 16)
        Q.log.append(["dma", key, 16])
        lst[i][1] = cnt + 16
        tok = ("dma", sem, cnt + 16, key)
        for b in reads:
            b.add_reader(tok)
        for b in writes:
            if b.w is not None and b.w[0] == "dma":
                b.wx = (b.wx + [b.w])[-3:]
            b.w = tok
            b.r = []
        return tok

    def barrier(self):
        self.nbar += 1
        for qn, Q, bsem in (("sp", self.sp, self.bar_sp), ("pool", self.pool, self.bar_pool)):
            for i, (sem, cnt) in enumerate(self.dsems[qn]):
                key = "d_%s_%d" % (qn, i)
                if cnt > 0 and Q.seen.get(key, 0) < cnt:
                    Q.eng.wait_ge(sem, cnt)
                    Q.log.append(["wait", key, cnt])
                    Q.seen[key] = cnt
        comp = [self.pe, self.act, self.dve]
        toks = [(E, E.n - 1) for E in comp if E.n > 0]
        for Q, bsem in ((self.sp, self.bar_sp), (self.pool, self.bar_pool)):
            for t in toks:
                Q.wait_token(t)
            Q.eng.sem_inc(bsem, 1)
            Q.log.append(["dma", "bar_" + Q.name, 1])
        for E in comp:
            for t in toks:
                if t[0] is not E:
                    E.wait_token(t)
            E.eng.wait_ge(self.bar_sp, self.nbar)
            E.eng.wait_ge(self.bar_pool, self.nbar)
            E.log.append(["wait", "bar_sp", self.nbar])
            E.log.append(["wait", "bar_pool", self.nbar])


DEBUG = False
SEQ_STREAMS = 0


def build_program():
    nc = bass.Bass("TRN2", target_bir_lowering=False)
    dt = nc.dram_tensor
    x_d = dt("x", [NSEQ, S, D], F32, kind="ExternalInput").ap()
    g1_d = dt("g1", [128, D], F32, kind="ExternalInput").ap()
    g2_d = dt("g2", [128, D], F32, kind="ExternalInput").ap()
    gf_d = dt("gf", [128, D], F32, kind="ExternalInput").ap()
    w_in_d = dt("w_in", [D, INW], F32, kind="ExternalInput").ap()
    w_sbu_d = dt("w_sb_up", [512, D], F32, kind="ExternalInput").ap()
    w_dlu_d = dt("w_dil_up", [256, D], F32, kind="ExternalInput").ap()
    w_out_d = dt("w_out", [D, D], F32, kind="ExternalInput").ap()
    w_fi_d = dt("w_ffn_in", [D, 2 * DFF], F32, kind="ExternalInput").ap()
    w_fo_d = dt("w_ffn_out", [DFF, D], F32, kind="ExternalInput").ap()
    cmat_d = dt("cmat", [128, 4, 128], F32, kind="ExternalInput").ap()
    dbias_d = dt("dbias", [128, 12, 256], F32, kind="ExternalInput").ap()
    out_d = dt("out", [NSEQ, S, D], F32, kind="ExternalOutput").ap()
    wfi_bf = dt("wfi_bf16", [NFF, 128, 8 * 2 * 128], BF16).ap()
    scr_b = [Buf() for _ in range(NFF)]
    if DEBUG:
        dbg_uT = dt("dbg_uT", [128, 8, S], BF16, kind="ExternalOutput").ap()
        dbg_osb = dt("dbg_osb", [128, 4, S], BF16, kind="ExternalOutput").ap()
        dbg_odl = dt("dbg_odl", [128, 2, S], BF16, kind="ExternalOutput").ap()
        dbg_mT = dt("dbg_mT", [128, 8, S], BF16, kind="ExternalOutput").ap()
        dbg_x1 = dt("dbg_x1", [4, 128, D], F32, kind="ExternalOutput").ap()
        dbg_acc = dt("dbg_acc", [128, 4, S], F32, kind="ExternalOutput").ap()

    es = contextlib.ExitStack()
    with es:
        cx = Ctx(nc, es)
        pe, act, dve = cx.pe, cx.act, cx.dve

        def sb(name, shape, dtype, stack=es):
            return stack.enter_context(nc.sbuf_tensor("sb_" + name, shape, dtype))

        pbank = [es.enter_context(nc.psum_tensor("ps%d" % i, [128, 512], F32)) for i in range(8)]
        pbuf = [Buf() for _ in range(8)]
        ptr = pbank[7].bitcast(BF16)
        ptr_b = pbuf[7]
        ptrs = [(pbank[6].bitcast(BF16), pbuf[6]), (pbank[7].bitcast(BF16), pbuf[7])]
        ptr_rr = [0]

        cmat = sb("cmat", [128, 4, 128], BF16)
        cmat_b = Buf()
        ones_bf = sb("ones_bf", [128, 128], BF16)
        negones_bf = sb("negones_bf", [128, 128], BF16)
        consts_b = Buf()
        g_b = Buf()
        wo = sb("wo", [128, 8, D], BF16)
        wo_b = Buf()
        junk = sb("junk", [128, D], BF16)
        junk_b = Buf()
        stat = sb("stat", [128, 64], F32)
        cx.dma("pool", cmat[:], cmat_d, writes=[cmat_b])
        cx.dma("pool", wo[:], w_out_d.rearrange("(k p) n -> p k n", p=128), writes=[wo_b])
        cx.op(dve, lambda: nc.vector.memset(ones_bf[:], 1.0), writes=[consts_b])
        cx.op(dve, lambda: nc.vector.memset(negones_bf[:], -1.0), writes=[consts_b])
        ident = cmat[:, 0, :]
        negtinc = cmat[:, 1, :]
        maskneg = cmat[:, 2, :]

        evac_rr = [0]

        def evac(out_ap, in_ap, reads, writes, scale=None, which=None):
            if which is None:
                which = "act" if (evac_rr[0] % 2 == 0) else "dve"
                evac_rr[0] += 1
            if which == "act":
                if scale is None:
                    return cx.op(act, lambda: nc.scalar.copy(out=out_ap, in_=in_ap), reads, writes)
                return cx.op(act, lambda: nc.scalar.mul(out=out_ap, in_=in_ap, mul=scale), reads, writes)
            if scale is None:
                return cx.op(dve, lambda: nc.vector.tensor_copy(out=out_ap, in_=in_ap), reads, writes)
            return cx.op(dve, lambda: nc.vector.tensor_scalar(out=out_ap, in0=in_ap, scalar1=scale, scalar2=None,
                                                              op0=ALU.mult), reads, writes)

        prr = [0]

        def next_bank(cands=(0, 1, 2, 3, 4, 5, 6)):
            i = cands[prr[0] % len(cands)]
            prr[0] += 1
            return i

        def wload(dst_ap, src_ap, buf):
            return cx.dma("pool", dst_ap, src_ap, writes=[buf])

        scr = {}

        def wload_c(key, seq, tile_flat, parts, buf):
            if seq == 0:
                for d_ap, s_ap in parts:
                    wload(d_ap, s_ap, buf)
                P_, F_ = tile_flat.shape
                sd = dt("scr_" + key, [P_, F_], BF16).ap()
                sbf = Buf()
                scr[key] = (sd, sbf)
                return ("store", sd, sbf)
            sd, sbf = scr[key]
            cx.dma("sp", tile_flat, sd, reads=[sbf], writes=[buf])
            return None

        def wstash(pending, tile_flat, buf):
            if pending is not None:
                _, sd, sbf = pending
                cx.dma("sp", sd, tile_flat, reads=[buf], writes=[sbf])

        def rms_stats(src_ap, src_b, col, stat_b=None):
            stat_b = stat_b or stat_b0
            cx.op(act, lambda: nc.scalar.activation(out=junk[:], in_=src_ap, func=AF.Square,
                                                    accum_out=stat[:, col:col + 1]),
                  reads=[src_b], writes=[junk_b, stat_b])

        stat_b0 = Buf()
        stat_b = stat_b0

        def rstd_batch(c0, n, cdst, stat_b=None):
            stat_b = stat_b or stat_b0
            cx.op(dve, lambda: nc.vector.tensor_scalar(out=stat[:, c0:c0 + n], in0=stat[:, c0:c0 + n],
                                                       scalar1=1.0 / D, scalar2=EPS, op0=ALU.mult, op1=ALU.add),
                  reads=[stat_b], writes=[stat_b])
            cx.op(act, lambda: nc.scalar.activation(out=stat[:, c0:c0 + n], in_=stat[:, c0:c0 + n], func=AF.Ln),
                  reads=[stat_b], writes=[stat_b])
            cx.op(act, lambda: nc.scalar.activation(out=stat[:, cdst:cdst + n], in_=stat[:, c0:c0 + n],
                                                    func=AF.Exp, scale=-0.5),
                  reads=[stat_b], writes=[stat_b])

        def norm_transpose(src_ap, src_b, rcol, gtile, ub, ub_b, dstT, dst_b, tcol, stat_b=None):
            stat_b = stat_b or stat_b0
            cx.op(dve, lambda: nc.vector.scalar_tensor_tensor(out=ub[:], in0=src_ap, scalar=stat[:, rcol:rcol + 1],
                                                              in1=gtile[:], op0=ALU.mult, op1=ALU.mult),
                  reads=[src_b, stat_b, g_b], writes=[ub_b])
            pt_, ptb_ = ptrs[ptr_rr[0] % 2]
            ptr_rr[0] += 1
            for c in range(8):
                cx.op(pe, lambda c=c: nc.tensor.transpose(pt_[:, c * 128:(c + 1) * 128], ub[:, c * 128:(c + 1) * 128],
                                                          ident),
                      reads=[ub_b, cmat_b], writes=[ptb_])
            evac(dstT[:, :, tcol:tcol + 128], pt_[:].rearrange("p (c n) -> p c n", c=8), [ptb_], [dst_b])

        for s in range(NSEQ):
            with contextlib.ExitStack() as es_seq:
                mT = sb("mT%d" % s, [128, 8, S], BF16, es_seq)
                mT_b = Buf()
                with contextlib.ExitStack() as es_att:
                    uT = sb("uT%d" % s, [128, 8, S], BF16, es_att)
                    uT_b = Buf()
                    osb = sb("osb%d" % s, [128, 4, S], BF16, es_att)
                    osb_b = Buf()
                    odl = sb("odl%d" % s, [128, 2, S], BF16, es_att)
                    odl_b = Buf()

                    with contextlib.ExitStack() as es_a:
                        xs = [sb("xs%d_%d" % (s, i), [128, D], F32, es_a) for i in range(8)]
                        xs_b = [Buf() for _ in range(8)]
                        ub = [sb("ub%d_%d" % (s, i), [128, D], BF16, es_a) for i in range(4)]
                        ub_b = [Buf() for _ in range(4)]
                        g1 = sb("g1_%d" % s, [128, D], F32, es_a)
                        cx.dma("sp", g1[:], g1_d, writes=[g_b])
                        sta_b = [Buf(), Buf()]

                        def a_front(g0):
                            sc = 4 * ((g0 // 4) % 2)
                            for i in range(g0, g0 + 4):
                                cx.dma("sp", xs[i % 8][:], x_d[s, i * 128:(i + 1) * 128, :], writes=[xs_b[i % 8]])
                            for i in range(g0, g0 + 4):
                                rms_stats(xs[i % 8][:], xs_b[i % 8], sc + i - g0, sta_b[(g0 // 4) % 2])
                            rstd_batch(sc, 4, 16 + sc, sta_b[(g0 // 4) % 2])

                        def a_back(g0):
                            sc = 4 * ((g0 // 4) % 2)
                            for i in range(g0, g0 + 4):
                                norm_transpose(xs[i % 8][:], xs_b[i % 8], 16 + sc + i - g0, g1, ub[i % 4], ub_b[i % 4],
                                               uT, uT_b, i * 128, sta_b[(g0 // 4) % 2])

                        a_front(0)
                        for g0 in range(0, NT, 4):
                            if g0 + 4 < NT:
                                a_front(g0 + 4)
                            a_back(g0)
                        if DEBUG and s == 0:
                            cx.dma("sp", dbg_uT, uT[:], reads=[uT_b])
                        cx.barrier()

                    with contextlib.ExitStack() as es_sb:
                        vsb = sb("vsb%d" % s, [128, NT, 512], BF16, es_sb)
                        vsb_b = Buf()
                        wv = sb("wv%d" % s, [128, 8, 512], BF16, es_sb)
                        wv_b = Buf()
                        wqk = [sb("wqk%d_%d" % (s, i), [128, 8, 2, 128], BF16, es_sb) for i in range(2)]
                        wqk_b = [Buf(), Buf()]
                        qTA = [sb("qTA%d_%d" % (s, i), [128, S], BF16, es_sb) for i in range(2)]
                        qTB = [sb("qTB%d_%d" % (s, i), [128, S], BF16, es_sb) for i in range(2)]
                        qpad_b = Buf()
                        for i_ in range(2):
                            cx.op(dve, lambda: nc.vector.memset(qTA[i_][64:128, :], 0.0), writes=[qpad_b])
                            cx.op(dve, lambda: nc.vector.memset(qTB[i_][0:64, :], 0.0), writes=[qpad_b])
                        kT = [sb("kT%d_%d" % (s, i), [128, S], BF16, es_sb) for i in range(2)]
                        qk_b = [Buf(), Buf()]
                        NSTR = 4
                        eb = [sb("eb%d_%d" % (s, i), [128, 512], F32, es_sb) for i in range(NSTR)]
                        eb_b = [Buf() for _ in range(NSTR)]
                        spb = [sb("spb%d_%d" % (s, i), [128, 512], BF16, es_sb) for i in range(NSTR)]
                        spb_b = [Buf() for _ in range(NSTR)]
                        atb = [sb("atb%d_%d" % (s, i), [128, 512], BF16, es_sb) for i in range(NSTR)]
                        atb_b = [Buf() for _ in range(NSTR)]
                        z0r = [sb("z0r%d_%d" % (s, i), [1, 512], F32, es_sb) for i in range(NSTR)]
                        z0r_b = [Buf() for _ in range(NSTR)]
                        cbfs = [sb("cbf%d_%d" % (s, i), [128, 512], BF16, es_sb) for i in range(NSTR)]
                        for i_ in range(NSTR):
                            cx.op(dve, lambda: nc.vector.memset(cbfs[i_][:], 0.0), writes=[qpad_b])
                        cbf_b = [Buf() for _ in range(NSTR)]

                        wv_flat = wv[:].rearrange("p c n -> p (c n)")
                        pend_wv = wload_c("wv", s, wv_flat, [(wv[:], w_in_d[:, 1024:1536].rearrange("(c p) n -> p c n", p=128))], wv_b)
                        for i in range(NT):
                            bk = next_bank((0, 1, 2, 3))
                            for c in range(8):
                                cx.op(pe, lambda c=c, bk=bk, i=i: nc.tensor.matmul(
                                    pbank[bk][:], lhsT=uT[:, c, i * 128:(i + 1) * 128], rhs=wv[:, c, :],
                                    start=(c == 0), stop=(c == 7)), reads=[uT_b, wv_b], writes=[pbuf[bk]])
                            evac(vsb[:, i, :], pbank[bk][:], [pbuf[bk]], [vsb_b])
                            if i == 0:
                                wstash(pend_wv, wv_flat, wv_b)

                        pend_qk = {}

                        def load_qk(hp):
                            pb_ = hp % 2
                            pend_qk[hp] = wload_c("wqk%d" % hp, s, wqk[pb_][:].rearrange("p c t n -> p (c t n)"), [
                                (wqk[pb_][:, :, 0, :], w_in_d[:, hp * 128:(hp + 1) * 128].rearrange("(c p) n -> p c n", p=128)),
                                (wqk[pb_][:, :, 1, :],
                                 w_in_d[:, 512 + hp * 128:512 + (hp + 1) * 128].rearrange("(c p) n -> p c n", p=128))], wqk_b[pb_])

                        def sb_stream(k, hp, hh, groups):
                            pb_ = hp % 2
                            h = 2 * hp + hh
                            R = slice(hh * 64, (hh + 1) * 64)
                            qTh = qTA[pb_] if hh == 0 else qTB[pb_]
                            kTh = kT[pb_]
                            bZA = k
                            bO = 4 + k
                            ob_ = pbuf[bO]
                            cbf = cbfs[k]
                            tp = (0, 64) if hh == 1 else None
                            units = []
                            for g in groups:
                                nkb = 4 * g + 4
                                for kb in range(nkb - 1, -1, -1):
                                    q0 = 512 * g
                                    units.append(dict(g=g, q0=q0, kb=kb, diag=(128 * kb >= q0), c0=max(q0, 128 * kb) - q0,
                                                      first=(kb == nkb - 1), last=(kb == 0)))

                            def s1(u):
                                q0, c0, kb = u["q0"], u["c0"], u["kb"]
                                ks = slice(kb * 128, (kb + 1) * 128)
                                if u["first"]:
                                    cx.op(dve, lambda: nc.vector.memset(cbf[0:1, :], 0.0), writes=[cbf_b[k]])
                                cx.op(pe, lambda: nc.tensor.matmul(
                                    pbank[bZA][:, c0:512], lhsT=kTh[:, ks],
                                    rhs=qTh[:, q0 + c0:q0 + 512], start=True, stop=False, skip_group_check=True),
                                    reads=[qk_b[pb_], qpad_b], writes=[pbuf[bZA]])
                                if u["diag"]:
                                    cx.op(pe, lambda: nc.tensor.matmul(
                                        pbank[bZA][:, c0:c0 + 128], lhsT=ident, rhs=maskneg,
                                        start=False, stop=False, skip_group_check=True),
                                        reads=[cmat_b], writes=[pbuf[bZA]])

                            def s2(u):
                                c0 = u["c0"]
                                cx.op(act, lambda: nc.scalar.activation(out=eb[k][:, c0:512], in_=pbank[bZA][:, c0:512], func=AF.Exp),
                                      reads=[pbuf[bZA]], writes=[eb_b[k]])

                            def s2b(u):
                                c0 = u["c0"]
                                cx.op(act, lambda: nc.scalar.activation(out=spb[k][:, c0:512], in_=eb[k][:, c0:512],
                                                                        func=AF.Ln, bias=1.0, scale=1.0),
                                      reads=[eb_b[k]], writes=[spb_b[k]])
                                if not u["last"]:
                                    cx.op(dve, lambda: nc.vector.tensor_copy(out=z0r[k][0:1, c0:512], in_=pbank[bZA][0:1, c0:512]),
                                          reads=[pbuf[bZA], eb_b[k]], writes=[z0r_b[k]])

                            def s3(u):
                                c0 = u["c0"]
                                cx.op(pe, lambda: nc.tensor.matmul(
                                    pbank[bZA][:, c0:512], lhsT=negtinc, rhs=spb[k][:, c0:512],
                                    start=False, stop=u["first"], skip_group_check=True),
                                    reads=[spb_b[k], cmat_b], writes=[pbuf[bZA]])
                                if not u["first"]:
                                    cx.op(pe, lambda: nc.tensor.matmul(
                                        pbank[bZA][:, c0:512], lhsT=negones_bf[:, :], rhs=cbf[:, c0:512],
                                        start=False, stop=True, skip_group_check=True),
                                        reads=[cbf_b[k], consts_b, qpad_b], writes=[pbuf[bZA]])

                            def s4(u):
                                c0 = u["c0"]
                                cx.op(act, lambda: nc.scalar.activation(out=atb[k][:, c0:512], in_=pbank[bZA][:, c0:512], func=AF.Exp),
                                      reads=[pbuf[bZA]], writes=[atb_b[k]])
                                if not u["last"]:
                                    cx.op(dve, lambda: nc.vector.tensor_tensor(
                                        out=cbf[0:1, c0:512], in0=z0r[k][0:1, c0:512], in1=pbank[bZA][0:1, c0:512],
                                        op=ALU.subtract), reads=[z0r_b[k], pbuf[bZA], atb_b[k]], writes=[cbf_b[k]])

                            def s5(u):
                                c0, kb, q0 = u["c0"], u["kb"], u["q0"]
                                vh = vsb[:, kb, hp * 128:(hp + 1) * 128]
                                cx.op(pe, lambda: nc.tensor.matmul(
                                    pbank[bO][:, c0:512], lhsT=vh, rhs=atb[k][:, c0:512],
                                    start=u["first"], stop=u["last"], skip_group_check=True),
                                    reads=[atb_b[k], vsb_b], writes=[ob_])
                                if u["last"]:
                                    evac(osb[R, hp, q0:q0 + 512], pbank[bO][R, :], [ob_], [osb_b], which="dve")

                            s1(units[0])
                            yield
                            for ui, u in enumerate(units):
                                s2(u)
                                yield
                                s2b(u)
                                yield
                                s3(u)
                                yield
                                s4(u)
                                yield
                                if ui + 1 < len(units):
                                    s1(units[ui + 1])
                                s5(u)
                                yield

                        load_qk(0)
                        for hp in range(4):
                            pb_ = hp % 2
                            for which, dst, scale in ((0, None, 0.125), (1, kT[pb_], None)):
                                for tg in range(4):
                                    bk = next_bank((0, 1, 2, 3))
                                    for c in range(8):
                                        cx.op(pe, lambda c=c, bk=bk, tg=tg, which=which: nc.tensor.matmul(
                                            pbank[bk][:], lhsT=wqk[pb_][:, c, which, :],
                                            rhs=uT[:, c, tg * 512:(tg + 1) * 512],
                                            start=(c == 0), stop=(c == 7)), reads=[uT_b, wqk_b[pb_]], writes=[pbuf[bk]])
                                    if which == 0:
                                        we_ = "act" if tg % 2 == 0 else "dve"
                                        evac(qTA[pb_][0:64, tg * 512:(tg + 1) * 512], pbank[bk][0:64, :], [pbuf[bk]], [qk_b[pb_]], scale=scale,
                                             which=we_)
                                        evac(qTB[pb_][64:128, tg * 512:(tg + 1) * 512], pbank[bk][64:128, :], [pbuf[bk]], [qk_b[pb_]], scale=scale,
                                             which=we_)
                                    else:
                                        evac(dst[:, tg * 512:(tg + 1) * 512], pbank[bk][:], [pbuf[bk]], [qk_b[pb_]], scale=scale)
                            wstash(pend_qk[hp], wqk[pb_][:].rearrange("p c t n -> p (c t n)"), wqk_b[pb_])
                            if hp + 1 < 4:
                                load_qk(hp + 1)
                            if s == 0 and PRECAST:
                                for j in range(hp * 6, min(NFF, hp * 6 + 6)):
                                    dst4 = wfi_bf[j].rearrange("p (k t n) -> p k t n", k=8, t=2)
                                    for t_ in range(2):
                                        cx.dma("pool", dst4[:, :, t_, :],
                                               w_fi_d[:, t_ * DFF + j * 128:t_ * DFF + (j + 1) * 128].rearrange("(k p) n -> p k n", p=128),
                                               writes=[scr_b[j]])
                            streams = [sb_stream(0, hp, 0, (0, 3)), sb_stream(1, hp, 1, (0, 3)),
                                       sb_stream(2, hp, 0, (1, 2)), sb_stream(3, hp, 1, (1, 2))]
                            live = list(streams)
                            if SEQ_STREAMS == 1:
                                for gen in streams:
                                    for _ in gen:
                                        pass
                                live = []
                            elif SEQ_STREAMS in (2, 3, 4, 5):
                                pairs = {2: ((0, 1), (2, 3)), 3: ((0, 2), (1, 3)), 4: ((0, 2), (1,), (3,)), 5: ((1, 3), (0,), (2,))}[SEQ_STREAMS]
                                for pr in pairs:
                                    lv = [streams[i_] for i_ in pr]
                                    while lv:
                                        nx = []
                                        for gen in lv:
                                            try:
                                                next(gen)
                                                nx.append(gen)
                                            except StopIteration:
                                                pass
                                        lv = nx
                                live = []
                            while live:
                                nxt = []
                                for gen in live:
                                    try:
                                        next(gen)
                                        nxt.append(gen)
                                    except StopIteration:
                                        pass
                                live = nxt
                        if DEBUG and s == 0:
                            cx.dma("sp", dbg_osb, osb[:], reads=[osb_b])
                        cx.barrier()

                    with contextlib.ExitStack() as es_dl:
                        acc = sb("acc%d" % s, [128, 4, S], F32, es_dl)
                        acc_b = Buf()
                        dbias = sb("dbias%d" % s, [128, 4, 256], F32, es_dl)
                        dbias_b = Buf()
                        wd = sb("wd%d" % s, [128, 8, 3, 256], BF16, es_dl)
                        wd_b = Buf()
                        qd = sb("qd%d" % s, [128, 2, S], BF16, es_dl)
                        kd = sb("kd%d" % s, [128, 2, S], BF16, es_dl)
                        qkd_b = Buf()
                        vd = sb("vd%d" % s, [128, NT, 4, 128], BF16, es_dl)
                        vd_b = Buf()
                        tb = [sb("tb%d_%d" % (s, i), [128, 256], F32, es_dl) for i in range(4)]
                        tb_b = [Buf() for _ in range(4)]
                        pbf = [sb("pbf%d_%d" % (s, i), [128, 256], BF16, es_dl) for i in range(4)]
                        pbf_b = [Buf() for _ in range(4)]
                        shf = sb("shf%d" % s, [128, 128], F32, es_dl)
                        den = sb("den%d" % s, [128, 512], F32, es_dl)
                        den_b = Buf()
                        rden = [sb("rden%d_%d" % (s, i), [128, 512], F32, es_dl) for i in range(2)]
                        rden_b = [Buf(), Buf()]
                        shf_b = Buf()
                        cx.dma("sp", shf[:], cmat_d[:, 3, :], writes=[shf_b])
                        for hh in range(4):
                            o1 = 64 if hh % 2 == 0 else 0
                            cx.op(dve, lambda: nc.vector.memset(vd[:, :, hh, o1:o1 + 64], 1.0), writes=[vd_b])
                        blk = [0]
                        nbat = [0]

                        pend_wd = {}
                        wd_flat = wd[:].rearrange("p c t n -> p (c t n)")

                        def load_wd(gi_):
                            pend_wd[gi_] = wload_c("wd%d" % gi_, s, wd_flat, [
                                (wd[:, :, j, :], w_in_d[:, base + 256 * gi_:base + 256 * (gi_ + 1)].rearrange("(c p) n -> p c n", p=128))
                                for j, base in enumerate((1536, 2304, 3072))], wd_b)

                        load_wd(0)
                        for gi in range(3):
                            d = DIL[gi]
                            nbr = NT // d
                            cx.dma("sp", dbias[:], dbias_d[:, 4 * gi:4 * gi + 4, :], writes=[dbias_b])
                            for which, dst, scale in ((0, qd, 0.125), (1, kd, None)):
                                for pp in range(2):
                                    for tg in range(4):
                                        bk = next_bank()
                                        for c in range(8):
                                            cx.op(pe, lambda c=c, bk=bk, tg=tg, which=which, pp=pp: nc.tensor.matmul(
                                                pbank[bk][:], lhsT=wd[:, c, which, pp * 128:(pp + 1) * 128],
                                                rhs=uT[:, c, tg * 512:(tg + 1) * 512],
                                                start=(c == 0), stop=(c == 7)), reads=[uT_b, wd_b], writes=[pbuf[bk]])
                                        evac(dst[:, pp, tg * 512:(tg + 1) * 512], pbank[bk][:], [pbuf[bk]], [qkd_b], scale=scale)

                            def sub(t3, r, m):
                                if d == 1:
                                    return t3[:, m * 128:(m + 1) * 128]
                                return t3.rearrange("p (i d) -> p d i", d=d)[:, r, m * 128:(m + 1) * 128]

                            for r in range(d):
                                for m in range(nbr):
                                    ti = r * nbr + m
                                    bk = next_bank()
                                    for c in range(8):
                                        cx.op(pe, lambda c=c, bk=bk, r=r, m=m: nc.tensor.matmul(
                                            pbank[bk][:, 0:256], lhsT=sub(uT[:, c, :], r, m), rhs=wd[:, c, 2, :],
                                            start=(c == 0), stop=(c == 7)), reads=[uT_b, wd_b], writes=[pbuf[bk]])
                                    we_ = "act" if ti % 2 == 0 else "dve"
                                    for par in range(2):
                                        evac(vd[:, ti, :, :].rearrange("p (a b) e -> p a b e", b=2)[:, :, par, par * 64:par * 64 + 64],
                                             pbank[bk][:, 0:256].rearrange("p (a b e) -> p a b e", a=2, b=2)[:, :, par, :],
                                             [pbuf[bk]], [vd_b], which=we_)
                            wstash(pend_wd[gi], wd_flat, wd_b)
                            if gi + 1 < 3:
                                load_wd(gi + 1)
                            for hh in range(4):
                                h = 4 * gi + hh
                                pp = hh // 2
                                R = slice((hh % 2) * 64, (hh % 2 + 1) * 64)
                                items = []
                                for r in range(d):
                                    for n0 in range(0, nbr, 4):
                                        nn = min(4, nbr - n0)
                                        bN = 5 + (nbat[0] % 2)
                                        nbat[0] += 1
                                        for n in range(n0, n0 + nn):
                                            items.append((r, n0, nn, n, bN))

                                def dl_scores(it, u):
                                    r, n0, nn, n, bN = it
                                    bS = u % 4
                                    wb = u % 4
                                    lo = 0 if n >= 1 else 128
                                    qv = sub(qd[R, pp, :], r, n)
                                    cx.op(pe, lambda: nc.tensor.matmul(
                                        pbank[bS][:, 128:256], lhsT=sub(kd[R, pp, :], r, n), rhs=qv,
                                        start=True, stop=True), reads=[qkd_b], writes=[pbuf[bS]])
                                    if n >= 1:
                                        cx.op(pe, lambda: nc.tensor.matmul(
                                            pbank[bS][:, 0:128], lhsT=sub(kd[R, pp, :], r, n - 1), rhs=qv,
                                            start=True, stop=True), reads=[qkd_b], writes=[pbuf[bS]])
                                    cx.op(dve, lambda: nc.vector.tensor_tensor(
                                        out=tb[wb][:, lo:256], in0=pbank[bS][:, lo:256], in1=dbias[:, hh, lo:256],
                                        op=ALU.add), reads=[pbuf[bS], dbias_b], writes=[tb_b[wb]])
                                    cx.op(act, lambda: nc.scalar.activation(out=pbf[wb][:, lo:256], in_=tb[wb][:, lo:256],
                                                                            func=AF.Exp),
                                          reads=[tb_b[wb]], writes=[pbf_b[wb]])

                                def dl_pv(it, u):
                                    r, n0, nn, n, bN = it
                                    wb = u % 4
                                    oc = (n - n0) * 128
                                    if n >= 1:
                                        cx.op(pe, lambda: nc.tensor.matmul(
                                            pbank[bN][:, oc:oc + 128], lhsT=vd[:, r * nbr + n - 1, hh, :],
                                            rhs=pbf[wb][:, 0:128], start=True, stop=False),
                                            reads=[pbf_b[wb], vd_b], writes=[pbuf[bN]])
                                    cx.op(pe, lambda: nc.tensor.matmul(
                                        pbank[bN][:, oc:oc + 128], lhsT=vd[:, r * nbr + n, hh, :],
                                        rhs=pbf[wb][:, 128:256], start=(n == 0), stop=True),
                                        reads=[pbf_b[wb], vd_b], writes=[pbuf[bN]])
                                    if n == n0 + nn - 1:
                                        if d == 1:
                                            av = acc[:, hh, n0 * 128:(n0 + nn) * 128]
                                        else:
                                            av = acc[:, hh, :].rearrange("p (i d) -> p d i", d=d)[:, r, n0 * 128:(n0 + nn) * 128]
                                        if gi == 0:
                                            cx.op(dve, lambda: nc.vector.tensor_copy(out=av, in_=pbank[bN][:, 0:nn * 128]),
                                                  reads=[pbuf[bN]], writes=[acc_b])
                                        else:
                                            cx.op(dve, lambda: nc.vector.tensor_tensor(out=av, in0=av, in1=pbank[bN][:, 0:nn * 128],
                                                                                       op=ALU.add),
                                                  reads=[pbuf[bN], acc_b], writes=[acc_b])

                                pend = []
                                for it in items:
                                    u = blk[0]
                                    blk[0] += 1
                                    dl_scores(it, u)
                                    pend.append((it, u))
                                    if len(pend) > 3:
                                        dl_pv(*pend.pop(0))
                                while pend:
                                    dl_pv(*pend.pop(0))
                        den2 = [den, rden[0]]
                        den2_b = [den_b, Buf()]
                        rd2 = sb("rden2_%d" % s, [128, 512], F32, es_dl)
                        rdn = [rden[1], rd2]
                        rdn_b = [rden_b[1], Buf()]

                        def nrm_ln(it):
                            hh, tg = divmod(it, 4)
                            PR = slice((hh % 2) * 64, (hh % 2 + 1) * 64)
                            bk = next_bank((0, 1, 2, 3))
                            cx.op(pe, lambda: nc.tensor.matmul(
                                pbank[bk][:, :], lhsT=shf[:, :], rhs=acc[:, hh, tg * 512:(tg + 1) * 512],
                                start=True, stop=True), reads=[acc_b, shf_b], writes=[pbuf[bk]])
                            cx.op(act, lambda: nc.scalar.activation(out=den2[it % 2][PR, :], in_=pbank[bk][PR, :], func=AF.Ln),
                                  reads=[pbuf[bk]], writes=[den2_b[it % 2]])

                        def nrm_exp(it):
                            hh, tg = divmod(it, 4)
                            PR = slice((hh % 2) * 64, (hh % 2 + 1) * 64)
                            di = it % 2
                            cx.op(act, lambda: nc.scalar.activation(out=rdn[di][PR, :], in_=den2[di][PR, :], func=AF.Exp, scale=-1.0),
                                  reads=[den2_b[di]], writes=[rdn_b[di]])
                            cx.op(dve, lambda: nc.vector.tensor_tensor(
                                out=odl[PR, hh // 2, tg * 512:(tg + 1) * 512], in0=acc[PR, hh, tg * 512:(tg + 1) * 512],
                                in1=rdn[di][PR, :], op=ALU.mult), reads=[rdn_b[di], acc_b], writes=[odl_b])

                        nrm_ln(0)
                        for it in range(16):
                            if it + 1 < 16:
                                nrm_ln(it + 1)
                            nrm_exp(it)
                        if DEBUG and s == 0:
                            cx.dma("sp", dbg_odl, odl[:], reads=[odl_b])
                            cx.dma("sp", dbg_acc, acc[:], reads=[acc_b])
                        cx.barrier()

                    with contextlib.ExitStack() as es_d1:
                        wg = [sb("wg%d_%d" % (s, i), [128, 8, 2, 128], BF16, es_d1) for i in range(2)]
                        wsu = [sb("wsu%d_%d" % (s, i), [128, 4, 128], BF16, es_d1) for i in range(2)]
                        wdu = [sb("wdu%d_%d" % (s, i), [128, 2, 128], BF16, es_d1) for i in range(2)]
                        w1_b = [Buf(), Buf()]
                        t1 = [sb("t1_%d_%d" % (s, i), [128, 512], F32, es_d1) for i in range(2)]
                        t2 = [sb("t2_%d_%d" % (s, i), [128, 512], F32, es_d1) for i in range(2)]
                        t_b = [Buf(), Buf()]
                        m1 = [sb("m1_%d_%d" % (s, i), [128, 512], F32, es_d1) for i in range(2)]
                        m2 = [sb("m2_%d_%d" % (s, i), [128, 512], F32, es_d1) for i in range(2)]
                        m_b = [Buf(), Buf()]
                        it = 0

                        pend_d1 = {}

                        def d1_flats(wi):
                            return (wg[wi][:].rearrange("p k t n -> p (k t n)"), wsu[wi][:].rearrange("p h n -> p (h n)"),
                                    wdu[wi][:].rearrange("p h n -> p (h n)"))

                        def load_d1(c):
                            wi = c % 2
                            f1, f2, f3 = d1_flats(wi)
                            pend_d1[c] = (
                                wload_c("wg%d" % c, s, f1, [
                                    (wg[wi][:, :, 0, :], w_in_d[:, 3840 + c * 128:3840 + (c + 1) * 128].rearrange("(k p) n -> p k n", p=128)),
                                    (wg[wi][:, :, 1, :], w_in_d[:, 4864 + c * 128:4864 + (c + 1) * 128].rearrange("(k p) n -> p k n", p=128))],
                                    w1_b[wi]),
                                wload_c("wsu%d" % c, s, f2, [(wsu[wi][:], w_sbu_d[:, c * 128:(c + 1) * 128].rearrange("(h p) n -> p h n", p=128))],
                                        w1_b[wi]),
                                wload_c("wdu%d" % c, s, f3, [(wdu[wi][:], w_dlu_d[:, c * 128:(c + 1) * 128].rearrange("(h p) n -> p h n", p=128))],
                                        w1_b[wi]))

                        load_d1(0)
                        for c in range(8):
                            wi = c % 2
                            if c + 1 < 8:
                                load_d1(c + 1)
                            stash_d1 = True
                            for tg in range(4):
                                ts_ = slice(tg * 512, (tg + 1) * 512)
                                b1, b2, b3, b4 = ((0, 1, 2, 3) if it % 2 == 0 else (4, 5, 6, 0))
                                if it % 2 == 1:
                                    b1, b2, b3, b4 = 4, 5, 6, 3
                                wk = it % 2
                                it += 1
                                for h in range(4):
                                    cx.op(pe, lambda h=h: nc.tensor.matmul(pbank[b1][:], lhsT=wsu[wi][:, h, :], rhs=osb[:, h, ts_],
                                                                           start=(h == 0), stop=(h == 3)),
                                          reads=[osb_b, w1_b[wi]], writes=[pbuf[b1]])
                                for h in range(2):
                                    cx.op(pe, lambda h=h: nc.tensor.matmul(pbank[b2][:], lhsT=wdu[wi][:, h, :], rhs=odl[:, h, ts_],
                                                                           start=(h == 0), stop=(h == 1)),
                                          reads=[odl_b, w1_b[wi]], writes=[pbuf[b2]])
                                for k in range(8):
                                    cx.op(pe, lambda k=k: nc.tensor.matmul(pbank[b3][:], lhsT=wg[wi][:, k, 0, :], rhs=uT[:, k, ts_],
                                                                           start=(k == 0), stop=(k == 7)),
                                          reads=[uT_b, w1_b[wi]], writes=[pbuf[b3]])
                                for k in range(8):
                                    cx.op(pe, lambda k=k: nc.tensor.matmul(pbank[b4][:], lhsT=wg[wi][:, k, 1, :], rhs=uT[:, k, ts_],
                                                                           start=(k == 0), stop=(k == 7)),
                                          reads=[uT_b, w1_b[wi]], writes=[pbuf[b4]])
                                cx.op(act, lambda: nc.scalar.activation(out=t1[wk][:], in_=pbank[b3][:], func=AF.Tanh, scale=0.5),
                                      reads=[pbuf[b3]], writes=[t_b[wk]])
                                cx.op(act, lambda: nc.scalar.activation(out=t2[wk][:], in_=pbank[b4][:], func=AF.Tanh, scale=0.5),
                                      reads=[pbuf[b4]], writes=[t_b[wk]])
                                cx.op(dve, lambda: nc.vector.scalar_tensor_tensor(out=m1[wk][:], in0=t1[wk][:], scalar=1.0,
                                                                                  in1=pbank[b1][:], op0=ALU.add, op1=ALU.mult),
                                      reads=[t_b[wk], pbuf[b1]], writes=[m_b[wk]])
                                cx.op(dve, lambda: nc.vector.scalar_tensor_tensor(out=m2[wk][:], in0=t2[wk][:], scalar=1.0,
                                                                                  in1=pbank[b2][:], op0=ALU.add, op1=ALU.mult),
                                      reads=[t_b[wk], pbuf[b2]], writes=[m_b[wk]])
                                cx.op(dve, lambda: nc.vector.tensor_tensor(out=mT[:, c, ts_], in0=m1[wk][:], in1=m2[wk][:], op=ALU.add),
                                      reads=[m_b[wk]], writes=[mT_b])
                                if stash_d1:
                                    stash_d1 = False
                                    for pnd, fl in zip(pend_d1[c], d1_flats(wi)):
                                        wstash(pnd, fl, w1_b[wi])
                        if DEBUG and s == 0:
                            cx.dma("sp", dbg_mT, mT[:], reads=[mT_b])
                        cx.barrier()

                with contextlib.ExitStack() as es_d:
                    g2 = sb("g2_%d" % s, [128, D], F32, es_d)
                    gf = sb("gf_%d" % s, [128, D], F32, es_d)
                    cx.dma("sp", g2[:], g2_d, writes=[g_b])
                    cx.dma("sp", gf[:], gf_d, writes=[g_b])
                    wfo = sb("wfo%d" % s, [128, NFF, D], BF16, es_d)
                    x1 = [[sb("x1_%d_%d_%d" % (s, p_, i), [128, D], F32, es_d) for i in range(4)] for p_ in range(2)]
                    x1_b = [[Buf() for _ in range(4)] for _ in range(2)]
                    u2T = [sb("u2T%d_%d" % (s, p_), [128, 8, 512], BF16, es_d) for p_ in range(2)]
                    u2T_b = [Buf(), Buf()]
                    hT = sb("hT%d" % s, [128, NFF, 512], BF16, es_d)
                    hT_b = [Buf() for _ in range(NFF)]
                    wf = [sb("wf%d_%d" % (s, i), [128, 8, 2, 128], BF16, es_d) for i in range(NWF)]
                    wf_b = [Buf() for _ in range(NWF)]
                    ub = [sb("ubd%d_%d" % (s, i), [128, D], BF16, es_d) for i in range(4)]
                    ub_b = [Buf() for _ in range(4)]
                    sg = [sb("sg%d_%d" % (s, i), [128, 512], F32, es_d) for i in range(2)]
                    sg_b = [Buf(), Buf()]
                    st2_b = Buf()
                    st3_b = Buf()
                    wfo_bs = [Buf() for _ in range(NFF // 2)]

                    def load_wfo(pi):
                        j0 = 2 * pi
                        fl = wfo[:, j0:j0 + 2, :].rearrange("p k n -> p (k n)")
                        pnd = wload_c("wfo%d" % pi, s, fl, [
                            (wfo[:, j0:j0 + 2, :], w_fo_d[j0 * 128:(j0 + 2) * 128, :].rearrange("(k p) n -> p k n", p=128))], wfo_bs[pi])
                        return pnd, fl
                    def load_wf(idx):
                        j = idx % NFF
                        wi = idx % NWF
                        flat = wf[wi][:].rearrange("p k t n -> p (k t n)")
                        if s == 0 and idx < NFF and not PRECAST:
                            wload(wf[wi][:, :, 0, :], w_fi_d[:, j * 128:(j + 1) * 128].rearrange("(k p) n -> p k n", p=128), wf_b[wi])
                            wload(wf[wi][:, :, 1, :],
                                  w_fi_d[:, DFF + j * 128:DFF + (j + 1) * 128].rearrange("(k p) n -> p k n", p=128), wf_b[wi])
                            cx.dma("sp", wfi_bf[j], flat, reads=[wf_b[wi]], writes=[scr_b[j]])
                        else:
                            cx.dma("sp", flat, wfi_bf[j], reads=[scr_b[j]], writes=[wf_b[wi]])

                    for i_ in range(PFD):
                        load_wf(i_)
                    wfo_pend = []
                    def d2_front(tq):
                        par = tq % 2
                        for i in range(4):
                            tok0 = tq * 512 + i * 128
                            cx.dma("sp", x1[par][i][:], x_d[s, tok0:tok0 + 128, :], writes=[x1_b[par][i]])
                            for half in range(2):
                                bk = next_bank()
                                for k in range(8):
                                    cx.op(pe, lambda k=k: nc.tensor.matmul(
                                        pbank[bk][:], lhsT=mT[:, k, tok0:tok0 + 128], rhs=wo[:, k, half * 512:(half + 1) * 512],
                                        start=(k == 0), stop=(k == 7)), reads=[mT_b, wo_b], writes=[pbuf[bk]])
                                hs_ = slice(half * 512, (half + 1) * 512)
                                cx.op(dve, lambda: nc.vector.scalar_tensor_tensor(
                                    out=x1[par][i][:, hs_], in0=pbank[bk][:], scalar=0.5,
                                    in1=x1[par][i][:, hs_], op0=ALU.mult, op1=ALU.add),
                                    reads=[pbuf[bk], x1_b[par][i]], writes=[x1_b[par][i]])
                            rms_stats(x1[par][i][:], x1_b[par][i], 32 + i, st2_b)
                        rstd_batch(32, 4, 36, st2_b)
                        if DEBUG and s == 0 and tq == 0:
                            for i in range(4):
                                cx.dma("sp", dbg_x1[i], x1[par][i][:], reads=[x1_b[par][i]])

                    def d2_scale(tq):
                        par = tq % 2
                        for i in range(4):
                            cx.op(dve, lambda: nc.vector.scalar_tensor_tensor(out=ub[i][:], in0=x1[par][i][:], scalar=stat[:, 36 + i:37 + i],
                                                                              in1=g2[:], op0=ALU.mult, op1=ALU.mult),
                                  reads=[x1_b[par][i], st2_b, g_b], writes=[ub_b[i]])

                    def d2_tr(tq):
                        par = tq % 2
                        for i in range(4):
                            pt_, ptb_ = ptrs[ptr_rr[0] % 2]
                            ptr_rr[0] += 1
                            for c in range(8):
                                cx.op(pe, lambda c=c: nc.tensor.transpose(pt_[:, c * 128:(c + 1) * 128], ub[i][:, c * 128:(c + 1) * 128],
                                                                          ident),
                                      reads=[ub_b[i], cmat_b], writes=[ptb_])
                            evac(u2T[par][:, :, i * 128:(i + 1) * 128], pt_[:].rearrange("p (c n) -> p c n", c=8), [ptb_], [u2T_b[par]])

                    def d5(tq):
                        par = tq % 2
                        for j in range(NFF):
                            idx = tq * NFF + j
                            wi = idx % NWF
                            if idx + PFD < 4 * NFF:
                                load_wf(idx + PFD)
                            if tq == 0 and j % 2 == 0:
                                pnd_, fl_ = load_wfo(j // 2)
                                if pnd_ is not None:
                                    wfo_pend.append((pnd_, fl_, wfo_bs[j // 2]))
                            if tq == 0 and j % 2 == 1 and wfo_pend and len(wfo_pend) > 2:
                                p0_, f0_, b0_ = wfo_pend.pop(0)
                                wstash(p0_, f0_, b0_)
                            bg = next_bank()
                            bu = next_bank()
                            for k in range(8):
                                cx.op(pe, lambda k=k: nc.tensor.matmul(pbank[bg][:], lhsT=wf[wi][:, k, 0, :], rhs=u2T[par][:, k, :],
                                                                       start=(k == 0), stop=(k == 7)),
                                      reads=[u2T_b[par], wf_b[wi]], writes=[pbuf[bg]])
                            for k in range(8):
                                cx.op(pe, lambda k=k: nc.tensor.matmul(pbank[bu][:], lhsT=wf[wi][:, k, 1, :], rhs=u2T[par][:, k, :],
                                                                       start=(k == 0), stop=(k == 7)),
                                      reads=[u2T_b[par], wf_b[wi]], writes=[pbuf[bu]])
                            cx.op(act, lambda: nc.scalar.activation(out=sg[j % 2][:], in_=pbank[bg][:], func=AF.Silu),
                                  reads=[pbuf[bg]], writes=[sg_b[j % 2]])
                            cx.op(dve, lambda: nc.vector.tensor_tensor(out=hT[:, j, :], in0=sg[j % 2][:], in1=pbank[bu][:], op=ALU.mult),
                                  reads=[sg_b[j % 2], pbuf[bu]], writes=[hT_b[j]])

                    def d6(tq):
                        par = tq % 2
                        for i in range(4):
                            for half in range(2):
                                bk = next_bank()
                                for j in range(NFF):
                                    cx.op(pe, lambda j=j: nc.tensor.matmul(
                                        pbank[bk][:], lhsT=hT[:, j, i * 128:(i + 1) * 128], rhs=wfo[:, j, half * 512:(half + 1) * 512],
                                        start=(j == 0), stop=(j == NFF - 1)), reads=[hT_b[j], wfo_bs[j // 2]], writes=[pbuf[bk]])
                                hs_ = slice(half * 512, (half + 1) * 512)
                                cx.op(dve, lambda: nc.vector.tensor_tensor(
                                    out=x1[par][i][:, hs_], in0=pbank[bk][:], in1=x1[par][i][:, hs_], op=ALU.add),
                                    reads=[pbuf[bk], x1_b[par][i]], writes=[x1_b[par][i]])
                            rms_stats(x1[par][i][:], x1_b[par][i], 40 + i, st3_b)
                        while wfo_pend:
                            p0_, f0_, b0_ = wfo_pend.pop(0)
                            wstash(p0_, f0_, b0_)

                    def d7(tq):
                        par = tq % 2
                        rstd_batch(40, 4, 44, st3_b)
                        for i in range(4):
                            tok0 = tq * 512 + i * 128
                            cx.op(dve, lambda: nc.vector.scalar_tensor_tensor(out=x1[par][i][:], in0=x1[par][i][:],
                                                                              scalar=stat[:, 44 + i:45 + i],
                                                                              in1=gf[:], op0=ALU.mult, op1=ALU.mult),
                                  reads=[x1_b[par][i], st3_b, g_b], writes=[x1_b[par][i]])
                            cx.dma("sp", out_d[s, tok0:tok0 + 128, :], x1[par][i][:], reads=[x1_b[par][i]])

                    d2_front(0)
                    d2_scale(0)
                    d2_tr(0)
                    for tq in range(4):
                        d5(tq)
                        if tq + 1 < 4:
                            d2_front(tq + 1)
                            d2_scale(tq + 1)
                        d6(tq)
                        if tq + 1 < 4:
                            d2_tr(tq + 1)
                        d7(tq)
                    cx.barrier()
    nc._cx = cx
    return nc


def _consts():
    cm = np.zeros((128, 4, 128), np.float32)
    cm[:, 0, :] = np.eye(128, dtype=np.float32)
    j = np.arange(128)[:, None]
    s_ = np.arange(128)[None, :]
    cm[:, 1, :] = np.where(j >= s_, -1.0, 0.0)
    cm[:, 2, :] = np.where(j < s_, 0.0, NEG)
    cm[:, 3, :] = (np.arange(128)[:, None] == ((np.arange(128) + 64) % 128)[None, :]).astype(np.float32)
    slopes = np.exp2(-8.0 * np.arange(1, 13, dtype=np.float64) / 12.0)
    db = np.full((128, 12, 256), NEG, np.float32)
    kc = np.arange(128)[:, None]
    qa = np.arange(128)[None, :]
    for h in range(12):
        cdil = slopes[h] * DIL[h // 4]
        prev = np.where(kc >= qa, -cdil * (128 + qa - kc), NEG)
        cur = np.where(kc <= qa, -cdil * (qa - kc), NEG)
        db[:, h, 0:128] = prev
        db[:, h, 128:256] = cur
    return cm, db


_CACHE = {}


def kernel(x, norm_mix_g, w_in, w_sb_up, w_dil_up, w_out, norm_ffn_g, w_ffn_in, w_ffn_out, norm_final_g):
    f = lambda a: np.ascontiguousarray(np.asarray(a, dtype=np.float32))
    x = f(x)
    n = 8
    if "nc" not in _CACHE:
        _CACHE["nc"] = build_program()
    nc = _CACHE["nc"]
    cm, db = _consts()
    bc = lambda g: np.ascontiguousarray(np.broadcast_to(f(g).reshape(1, D), (128, D)))
    shared = {
        "g1": bc(norm_mix_g), "g2": bc(norm_ffn_g), "gf": bc(norm_final_g),
        "w_in": f(w_in).reshape(D, INW), "w_sb_up": f(w_sb_up).reshape(512, D),
        "w_dil_up": f(w_dil_up).reshape(256, D), "w_out": f(w_out).reshape(D, D),
        "w_ffn_in": f(w_ffn_in).reshape(D, 2 * DFF), "w_ffn_out": f(w_ffn_out).reshape(DFF, D),
        "cmat": cm, "dbias": db,
    }
    in_maps = []
    for i in range(n):
        m = dict(shared)
        m["x"] = np.ascontiguousarray(x[NSEQ * i:NSEQ * (i + 1)])
        in_maps.append(m)
    res = run_bass_kernel_spmd(nc, in_maps, core_ids=list(range(n)))
    return np.concatenate([r["out"] for r in res.results], axis=0).astype(np.float32)
```

```python
import bisect
import contextlib
import math

import numpy as np
import concourse.bass as bass
import concourse.mybir as mybir
from concourse.bass_utils import run_bass_kernel_spmd

F32 = mybir.dt.float32
BF16 = mybir.dt.bfloat16
AF = mybir.ActivationFunctionType
ALU = mybir.AluOpType

D = 1024
S = 2048
NSEQ = 2
NT = S // 128
DFF = 2816
NFF = DFF // 128
INW = 5888
EPS = 1e-6
NEG = -30000.0
PRECAST = True
NWF = 4
PFD = 3
DIL = (1, 4, 16)


class Eng:
    def __init__(self, nc, eng, name, es, is_pe=False):
        self.eng = eng
        self.name = name
        self.sem = es.enter_context(nc.semaphore("sem_" + name))
        self.n = 0
        self.last = None
        self.mark_idx = []
        self.mark_cnt = []
        self.count = 0
        self.seen = {}
        self.is_pe = is_pe
        self.log = []
        self.instrs = []
        self.last_entry = None

    def emit(self, ins):
        self.last = ins
        self.n += 1
        self.last_entry = ["instr", self.n - 1, None]
        self.log.append(self.last_entry)
        self.instrs.append((ins, self.last_entry))
        return (self, self.n - 1)

    def threshold(self, idx):
        i = bisect.bisect_left(self.mark_idx, idx)
        if i < len(self.mark_idx):
            return self.mark_cnt[i]
        ins, ent = self.instrs[idx]
        ins.then_inc(self.sem, 1)
        ent[2] = self.name
        self.count += 1
        self.mark_idx.append(idx)
        self.mark_cnt.append(self.count)
        return self.count

    def wait_token(self, tok):
        if tok is None:
            return
        if tok[0] == "dma":
            _, sem, val, key = tok
        else:
            prod, idx = tok
            if prod is self and self.is_pe:
                return
            val = prod.threshold(idx)
            sem = prod.sem
            key = prod.name
        if self.seen.get(key, 0) >= val:
            return
        self.eng.wait_ge(sem, val)
        self.log.append(["wait", key, val])
        self.seen[key] = val


class Buf:
    __slots__ = ("w", "r", "wx")

    def __init__(self):
        self.w = None
        self.r = []
        self.wx = []

    def add_reader(self, tok):
        if tok[0] != "dma":
            self.r = [t for t in self.r if t[0] == "dma" or t[0] is not tok[0]]
        self.r.append(tok)


class Ctx:
    def __init__(self, nc, es):
        self.nc = nc
        self.pe = Eng(nc, nc.tensor, "pe", es, is_pe=True)
        self.act = Eng(nc, nc.scalar, "act", es)
        self.dve = Eng(nc, nc.vector, "dve", es)
        self.sp = Eng(nc, nc.sync, "sp", es)
        self.pool = Eng(nc, nc.gpsimd, "pool", es)
        self.dsems = {}
        for q in ("sp", "pool"):
            self.dsems[q] = [[es.enter_context(nc.semaphore("dsem_%s_%d" % (q, i))), 0] for i in range(12)]
        self.drr = {"sp": 0, "pool": 0}
        self.bar_sp = es.enter_context(nc.semaphore("bar_sp"))
        self.bar_pool = es.enter_context(nc.semaphore("bar_pool"))
        self.nbar = 0

    def _deps(self, E, reads, writes):
        for b in reads:
            E.wait_token(b.w)
            for t in b.wx:
                E.wait_token(t)
        for b in writes:
            E.wait_token(b.w)
            for t in b.wx:
                E.wait_token(t)
            for r in b.r:
                E.wait_token(r)

    def op(self, E, ins_fn, reads=(), writes=()):
        self._deps(E, reads, writes)
        ins = ins_fn()
        tok = E.emit(ins)
        for b in reads:
            b.add_reader(tok)
        for b in writes:
            b.w = tok
            b.wx = []
            b.r = []
        return tok

    def dma(self, qname, out, in_, reads=(), writes=()):
        Q = self.sp if qname == "sp" else self.pool
        self._deps(Q, reads, writes)
        lst = self.dsems[qname]
        i = self.drr[qname]
        self.drr[qname] = (i + 1) % len(lst)
        sem, cnt = lst[i]
        key = "d_%s_%d" % (qname, i)
        if cnt > 0 and Q.seen.get(key, 0) < cnt:
            Q.eng.wait_ge(sem, cnt)
            Q.log.append(["wait", key, cnt])
            Q.seen[key] = cnt
        Q.eng.dma_start(out=out, in_=in_).then_inc(sem, 16)
        Q.log.append(["dma", key, 16])
        lst[i][1] = cnt + 16
        tok = ("dma", sem, cnt + 16, key)
        for b in reads:
            b.add_reader(tok)
        for b in writes:
            if b.w is not None and b.w[0] == "dma":
                b.wx = (b.wx + [b.w])[-3:]
            b.w = tok
            b.r = []
        return tok

    def barrier(self):
        self.nbar += 1
        for qn, Q, bsem in (("sp", self.sp, self.bar_sp), ("pool", self.pool, self.bar_pool)):
            for i, (sem, cnt) in enumerate(self.dsems[qn]):
                key = "d_%s_%d" % (qn, i)
                if cnt > 0 and Q.seen.get(key, 0) < cnt:
                    Q.eng.wait_ge(sem, cnt)
                    Q.log.append(["wait", key, cnt])
                    Q.seen[key] = cnt
        comp = [self.pe, self.act, self.dve]
        toks = [(E, E.n - 1) for E in comp if E.n > 0]
        for Q, bsem in ((self.sp, self.bar_sp), (self.pool, self.bar_pool)):
            for t in toks:
                Q.wait_token(t)
            Q.eng.sem_inc(bsem, 1)
            Q.log.append(["dma", "bar_" + Q.name, 1])
        for E in comp:
            for t in toks:
                if t[0] is not E:
                    E.wait_token(t)
            E.eng.wait_ge(self.bar_sp, self.nbar)
            E.eng.wait_ge(self.bar_pool, self.nbar)
            E.log.append(["wait", "bar_sp", self.nbar])
            E.log.append(["wait", "bar_pool", self.nbar])


DEBUG = False
SEQ_STREAMS = 0


def build_program():
    nc = bass.Bass("TRN2", target_bir_lowering=False)
    dt = nc.dram_tensor
    x_d = dt("x", [NSEQ, S, D], F32, kind="ExternalInput").ap()
    g1_d = dt("g1", [128, D], F32, kind="ExternalInput").ap()
    g2_d = dt("g2", [128, D], F32, kind="ExternalInput").ap()
    gf_d = dt("gf", [128, D], F32, kind="ExternalInput").ap()
    w_in_d = dt("w_in", [D, INW], F32, kind="ExternalInput").ap()
    w_sbu_d = dt("w_sb_up", [512, D], F32, kind="ExternalInput").ap()
    w_dlu_d = dt("w_dil_up", [256, D], F32, kind="ExternalInput").ap()
    w_out_d = dt("w_out", [D, D], F32, kind="ExternalInput").ap()
    w_fi_d = dt("w_ffn_in", [D, 2 * DFF], F32, kind="ExternalInput").ap()
    w_fo_d = dt("w_ffn_out", [DFF, D], F32, kind="ExternalInput").ap()
    cmat_d = dt("cmat", [128, 4, 128], F32, kind="ExternalInput").ap()
    dbias_d = dt("dbias", [128, 12, 256], F32, kind="ExternalInput").ap()
    out_d = dt("out", [NSEQ, S, D], F32, kind="ExternalOutput").ap()
    wfi_bf = dt("wfi_bf16", [NFF, 128, 8 * 2 * 128], BF16).ap()
    scr_b = [Buf() for _ in range(NFF)]
    if DEBUG:
        dbg_uT = dt("dbg_uT", [128, 8, S], BF16, kind="ExternalOutput").ap()
        dbg_osb = dt("dbg_osb", [128, 4, S], BF16, kind="ExternalOutput").ap()
        dbg_odl = dt("dbg_odl", [128, 2, S], BF16, kind="ExternalOutput").ap()
        dbg_mT = dt("dbg_mT", [128, 8, S], BF16, kind="ExternalOutput").ap()
        dbg_x1 = dt("dbg_x1", [4, 128, D], F32, kind="ExternalOutput").ap()
        dbg_acc = dt("dbg_acc", [128, 4, S], F32, kind="ExternalOutput").ap()

    es = contextlib.ExitStack()
    with es:
        cx = Ctx(nc, es)
        pe, act, dve = cx.pe, cx.act, cx.dve

        def sb(name, shape, dtype, stack=es):
            return stack.enter_context(nc.sbuf_tensor("sb_" + name, shape, dtype))

        pbank = [es.enter_context(nc.psum_tensor("ps%d" % i, [128, 512], F32)) for i in range(8)]
        pbuf = [Buf() for _ in range(8)]
        ptr = pbank[7].bitcast(BF16)
        ptr_b = pbuf[7]
        ptrs = [(pbank[6].bitcast(BF16), pbuf[6]), (pbank[7].bitcast(BF16), pbuf[7])]
        ptr_rr = [0]

        cmat = sb("cmat", [128, 4, 128], BF16)
        cmat_b = Buf()
        ones_bf = sb("ones_bf", [128, 128], BF16)
        negones_bf = sb("negones_bf", [128, 128], BF16)
        consts_b = Buf()
        g_b = Buf()
        wo = sb("wo", [128, 8, D], BF16)
        wo_b = Buf()
        junk = sb("junk", [128, D], BF16)
        junk_b = Buf()
        stat = sb("stat", [128, 64], F32)
        cx.dma("pool", cmat[:], cmat_d, writes=[cmat_b])
        cx.dma("pool", wo[:], w_out_d.rearrange("(k p) n -> p k n", p=128), writes=[wo_b])
        cx.op(dve, lambda: nc.vector.memset(ones_bf[:], 1.0), writes=[consts_b])
        cx.op(dve, lambda: nc.vector.memset(negones_bf[:], -1.0), writes=[consts_b])
        ident = cmat[:, 0, :]
        negtinc = cmat[:, 1, :]
        maskneg = cmat[:, 2, :]

        evac_rr = [0]

        def evac(out_ap, in_ap, reads, writes, scale=None, which=None):
            if which is None:
                which = "act" if (evac_rr[0] % 2 == 0) else "dve"
                evac_rr[0] += 1
            if which == "act":
                if scale is None:
                    return cx.op(act, lambda: nc.scalar.copy(out=out_ap, in_=in_ap), reads, writes)
                return cx.op(act, lambda: nc.scalar.mul(out=out_ap, in_=in_ap, mul=scale), reads, writes)
            if scale is None:
                return cx.op(dve, lambda: nc.vector.tensor_copy(out=out_ap, in_=in_ap), reads, writes)
            return cx.op(dve, lambda: nc.vector.tensor_scalar(out=out_ap, in0=in_ap, scalar1=scale, scalar2=None,
                                                              op0=ALU.mult), reads, writes)

        prr = [0]

        def next_bank(cands=(0, 1, 2, 3, 4, 5, 6)):
            i = cands[prr[0] % len(cands)]
            prr[0] += 1
            return i

        def wload(dst_ap, src_ap, buf):
            return cx.dma("pool", dst_ap, src_ap, writes=[buf])

        scr = {}

        def wload_c(key, seq, tile_flat, parts, buf):
            if seq == 0:
                for d_ap, s_ap in parts:
                    wload(d_ap, s_ap, buf)
                P_, F_ = tile_flat.shape
                sd = dt("scr_" + key, [P_, F_], BF16).ap()
                sbf = Buf()
                scr[key] = (sd, sbf)
                return ("store", sd, sbf)
            sd, sbf = scr[key]
            cx.dma("sp", tile_flat, sd, reads=[sbf], writes=[buf])
            return None

        def wstash(pending, tile_flat, buf):
            if pending is not None:
                _, sd, sbf = pending
                cx.dma("sp", sd, tile_flat, reads=[buf], writes=[sbf])

        def rms_stats(src_ap, src_b, col, stat_b=None):
            stat_b = stat_b or stat_b0
            cx.op(act, lambda: nc.scalar.activation(out=junk[:], in_=src_ap, func=AF.Square,
                                                    accum_out=stat[:, col:col + 1]),
                  reads=[src_b], writes=[junk_b, stat_b])

        stat_b0 = Buf()
        stat_b = stat_b0

        def rstd_batch(c0, n, cdst, stat_b=None):
            stat_b = stat_b or stat_b0
            cx.op(dve, lambda: nc.vector.tensor_scalar(out=stat[:, c0:c0 + n], in0=stat[:, c0:c0 + n],
                                                       scalar1=1.0 / D, scalar2=EPS, op0=ALU.mult, op1=ALU.add),
                  reads=[stat_b], writes=[stat_b])
            cx.op(act, lambda: nc.scalar.activation(out=stat[:, c0:c0 + n], in_=stat[:, c0:c0 + n], func=AF.Ln),
                  reads=[stat_b], writes=[stat_b])
            cx.op(act, lambda: nc.scalar.activation(out=stat[:, cdst:cdst + n], in_=stat[:, c0:c0 + n],
                                                    func=AF.Exp, scale=-0.5),
                  reads=[stat_b], writes=[stat_b])

        def norm_transpose(src_ap, src_b, rcol, gtile, ub, ub_b, dstT, dst_b, tcol, stat_b=None):
            stat_b = stat_b or stat_b0
            cx.op(dve, lambda: nc.vector.scalar_tensor_tensor(out=ub[:], in0=src_ap, scalar=stat[:, rcol:rcol + 1],
                                                              in1=gtile[:], op0=ALU.mult, op1=ALU.mult),
                  reads=[src_b, stat_b, g_b], writes=[ub_b])
            pt_, ptb_ = ptrs[ptr_rr[0] % 2]
            ptr_rr[0] += 1
            for c in range(8):
                cx.op(pe, lambda c=c: nc.tensor.transpose(pt_[:, c * 128:(c + 1) * 128], ub[:, c * 128:(c + 1) * 128],
                                                          ident),
                      reads=[ub_b, cmat_b], writes=[ptb_])
            evac(dstT[:, :, tcol:tcol + 128], pt_[:].rearrange("p (c n) -> p c n", c=8), [ptb_], [dst_b])

        for s in range(NSEQ):
            with contextlib.ExitStack() as es_seq:
                mT = sb("mT%d" % s, [128, 8, S], BF16, es_seq)
                mT_b = Buf()
                with contextlib.ExitStack() as es_att:
                    uT = sb("uT%d" % s, [128, 8, S], BF16, es_att)
                    uT_b = Buf()
                    osb = sb("osb%d" % s, [128, 4, S], BF16, es_att)
                    osb_b = Buf()
                    odl = sb("odl%d" % s, [128, 2, S], BF16, es_att)
                    odl_b = Buf()

                    with contextlib.ExitStack() as es_a:
                        xs = [sb("xs%d_%d" % (s, i), [128, D], F32, es_a) for i in range(8)]
                        xs_b = [Buf() for _ in range(8)]
                        ub = [sb("ub%d_%d" % (s, i), [128, D], BF16, es_a) for i in range(4)]
                        ub_b = [Buf() for _ in range(4)]
                        g1 = sb("g1_%d" % s, [128, D], F32, es_a)
                        cx.dma("sp", g1[:], g1_d, writes=[g_b])
                        sta_b = [Buf(), Buf()]

                        def a_front(g0):
                            sc = 4 * ((g0 // 4) % 2)
                            for i in range(g0, g0 + 4):
                                cx.dma("sp", xs[i % 8][:], x_d[s, i * 128:(i + 1) * 128, :], writes=[xs_b[i % 8]])
                            for i in range(g0, g0 + 4):
                                rms_stats(xs[i % 8][:], xs_b[i % 8], sc + i - g0, sta_b[(g0 // 4) % 2])
                            rstd_batch(sc, 4, 16 + sc, sta_b[(g0 // 4) % 2])

                        def a_back(g0):
                            sc = 4 * ((g0 // 4) % 2)
                            for i in range(g0, g0 + 4):
                                norm_transpose(xs[i % 8][:], xs_b[i % 8], 16 + sc + i - g0, g1, ub[i % 4], ub_b[i % 4],
                                               uT, uT_b, i * 128, sta_b[(g0 // 4) % 2])

                        a_front(0)
                        for g0 in range(0, NT, 4):
                            if g0 + 4 < NT:
                                a_front(g0 + 4)
                            a_back(g0)
                        if DEBUG and s == 0:
                            cx.dma("sp", dbg_uT, uT[:], reads=[uT_b])
                        cx.barrier()

                    with contextlib.ExitStack() as es_sb:
                        vsb = sb("vsb%d" % s, [128, NT, 512], BF16, es_sb)
                        vsb_b = Buf()
                        wv = sb("wv%d" % s, [128, 8, 512], BF16, es_sb)
                        wv_b = Buf()
                        wqk = [sb("wqk%d_%d" % (s, i), [128, 8, 2, 128], BF16, es_sb) for i in range(2)]
                        wqk_b = [Buf(), Buf()]
                        qTA = [sb("qTA%d_%d" % (s, i), [128, S], BF16, es_sb) for i in range(2)]
                        qTB = [sb("qTB%d_%d" % (s, i), [128, S], BF16, es_sb) for i in range(2)]
                        qpad_b = Buf()
                        for i_ in range(2):
                            cx.op(dve, lambda: nc.vector.memset(qTA[i_][64:128, :], 0.0), writes=[qpad_b])
                            cx.op(dve, lambda: nc.vector.memset(qTB[i_][0:64, :], 0.0), writes=[qpad_b])
                        kT = [sb("kT%d_%d" % (s, i), [128, S], BF16, es_sb) for i in range(2)]
                        qk_b = [Buf(), Buf()]
                        NSTR = 4
                        eb = [sb("eb%d_%d" % (s, i), [128, 512], F32, es_sb) for i in range(NSTR)]
                        eb_b = [Buf() for _ in range(NSTR)]
                        spb = [sb("spb%d_%d" % (s, i), [128, 512], BF16, es_sb) for i in range(NSTR)]
                        spb_b = [Buf() for _ in range(NSTR)]
                        atb = [sb("atb%d_%d" % (s, i), [128, 512], BF16, es_sb) for i in range(NSTR)]
                        atb_b = [Buf() for _ in range(NSTR)]
                        z0r = [sb("z0r%d_%d" % (s, i), [1, 512], F32, es_sb) for i in range(NSTR)]
                        z0r_b = [Buf() for _ in range(NSTR)]
                        cbfs = [sb("cbf%d_%d" % (s, i), [128, 512], BF16, es_sb) for i in range(NSTR)]
                        for i_ in range(NSTR):
                            cx.op(dve, lambda: nc.vector.memset(cbfs[i_][:], 0.0), writes=[qpad_b])
                        cbf_b = [Buf() for _ in range(NSTR)]

                        wv_flat = wv[:].rearrange("p c n -> p (c n)")
                        pend_wv = wload_c("wv", s, wv_flat, [(wv[:], w_in_d[:, 1024:1536].rearrange("(c p) n -> p c n", p=128))], wv_b)
                        for i in range(NT):
                            bk = next_bank((0, 1, 2, 3))
                            for c in range(8):
                                cx.op(pe, lambda c=c, bk=bk, i=i: nc.tensor.matmul(
                                    pbank[bk][:], lhsT=uT[:, c, i * 128:(i + 1) * 128], rhs=wv[:, c, :],
                                    start=(c == 0), stop=(c == 7)), reads=[uT_b, wv_b], writes=[pbuf[bk]])
                            evac(vsb[:, i, :], pbank[bk][:], [pbuf[bk]], [vsb_b])
                            if i == 0:
                                wstash(pend_wv, wv_flat, wv_b)

                        pend_qk = {}

                        def load_qk(hp):
                            pb_ = hp % 2
                            pend_qk[hp] = wload_c("wqk%d" % hp, s, wqk[pb_][:].rearrange("p c t n -> p (c t n)"), [
                                (wqk[pb_][:, :, 0, :], w_in_d[:, hp * 128:(hp + 1) * 128].rearrange("(c p) n -> p c n", p=128)),
                                (wqk[pb_][:, :, 1, :],
                                 w_in_d[:, 512 + hp * 128:512 + (hp + 1) * 128].rearrange("(c p) n -> p c n", p=128))], wqk_b[pb_])

                        def sb_stream(k, hp, hh, groups):
                            pb_ = hp % 2
                            h = 2 * hp + hh
                            R = slice(hh * 64, (hh + 1) * 64)
                            qTh = qTA[pb_] if hh == 0 else qTB[pb_]
                            kTh = kT[pb_]
                            bZA = k
                            bO = 4 + k
                            ob_ = pbuf[bO]
                            cbf = cbfs[k]
                            tp = (0, 64) if hh == 1 else None
                            units = []
                            for g in groups:
                                nkb = 4 * g + 4
                                for kb in range(nkb - 1, -1, -1):
                                    q0 = 512 * g
                                    units.append(dict(g=g, q0=q0, kb=kb, diag=(128 * kb >= q0), c0=max(q0, 128 * kb) - q0,
                                                      first=(kb == nkb - 1), last=(kb == 0)))

                            def s1(u):
                                q0, c0, kb = u["q0"], u["c0"], u["kb"]
                                ks = slice(kb * 128, (kb + 1) * 128)
                                if u["first"]:
                                    cx.op(dve, lambda: nc.vector.memset(cbf[0:1, :], 0.0), writes=[cbf_b[k]])
                                cx.op(pe, lambda: nc.tensor.matmul(
                                    pbank[bZA][:, c0:512], lhsT=kTh[:, ks],
                                    rhs=qTh[:, q0 + c0:q0 + 512], start=True, stop=False, skip_group_check=True),
                                    reads=[qk_b[pb_], qpad_b], writes=[pbuf[bZA]])
                                if u["diag"]:
                                    cx.op(pe, lambda: nc.tensor.matmul(
                                        pbank[bZA][:, c0:c0 + 128], lhsT=ident, rhs=maskneg,
                                        start=False, stop=False, skip_group_check=True),
                                        reads=[cmat_b], writes=[pbuf[bZA]])

                            def s2(u):
                                c0 = u["c0"]
                                cx.op(act, lambda: nc.scalar.activation(out=eb[k][:, c0:512], in_=pbank[bZA][:, c0:512], func=AF.Exp),
                                      reads=[pbuf[bZA]], writes=[eb_b[k]])

                            def s2b(u):
                                c0 = u["c0"]
                                cx.op(act, lambda: nc.scalar.activation(out=spb[k][:, c0:512], in_=eb[k][:, c0:512],
                                                                        func=AF.Ln, bias=1.0, scale=1.0),
                                      reads=[eb_b[k]], writes=[spb_b[k]])
                                if not u["last"]:
                                    cx.op(dve, lambda: nc.vector.tensor_copy(out=z0r[k][0:1, c0:512], in_=pbank[bZA][0:1, c0:512]),
                                          reads=[pbuf[bZA], eb_b[k]], writes=[z0r_b[k]])

                            def s3(u):
                                c0 = u["c0"]
                                cx.op(pe, lambda: nc.tensor.matmul(
                                    pbank[bZA][:, c0:512], lhsT=negtinc, rhs=spb[k][:, c0:512],
                                    start=False, stop=u["first"], skip_group_check=True),
                                    reads=[spb_b[k], cmat_b], writes=[pbuf[bZA]])
                                if not u["first"]:
                                    cx.op(pe, lambda: nc.tensor.matmul(
                                        pbank[bZA][:, c0:512], lhsT=negones_bf[:, :], rhs=cbf[:, c0:512],
                                        start=False, stop=True, skip_group_check=True),
                                        reads=[cbf_b[k], consts_b, qpad_b], writes=[pbuf[bZA]])

                            def s4(u):
                                c0 = u["c0"]
                                cx.op(act, lambda: nc.scalar.activation(out=atb[k][:, c0:512], in_=pbank[bZA][:, c0:512], func=AF.Exp),
                                      reads=[pbuf[bZA]], writes=[atb_b[k]])
                                if not u["last"]:
                                    cx.op(dve, lambda: nc.vector.tensor_tensor(
                                        out=cbf[0:1, c0:512], in0=z0r[k][0:1, c0:512], in1=pbank[bZA][0:1, c0:512],
                                        op=ALU.subtract), reads=[z0r_b[k], pbuf[bZA], atb_b[k]], writes=[cbf_b[k]])

                            def s5(u):
                                c0, kb, q0 = u["c0"], u["kb"], u["q0"]
                                vh = vsb[:, kb, hp * 128:(hp + 1) * 128]
                                cx.op(pe, lambda: nc.tensor.matmul(
                                    pbank[bO][:, c0:512], lhsT=vh, rhs=atb[k][:, c0:512],
                                    start=u["first"], stop=u["last"], skip_group_check=True),
                                    reads=[atb_b[k], vsb_b], writes=[ob_])
                                if u["last"]:
                                    evac(osb[R, hp, q0:q0 + 512], pbank[bO][R, :], [ob_], [osb_b], which="dve")

                            s1(units[0])
                            yield
                            for ui, u in enumerate(units):
                                s2(u)
                                yield
                                s2b(u)
                                yield
                                s3(u)
                                yield
                                s4(u)
                                yield
                                if ui + 1 < len(units):
                                    s1(units[ui + 1])
                                s5(u)
                                yield

                        load_qk(0)
                        for hp in range(4):
                            pb_ = hp % 2
                            for which, dst, scale in ((0, None, 0.125), (1, kT[pb_], None)):
                                for tg in range(4):
                                    bk = next_bank((0, 1, 2, 3))
                                    for c in range(8):
                                        cx.op(pe, lambda c=c, bk=bk, tg=tg, which=which: nc.tensor.matmul(
                                            pbank[bk][:], lhsT=wqk[pb_][:, c, which, :],
                                            rhs=uT[:, c, tg * 512:(tg + 1) * 512],
                                            start=(c == 0), stop=(c == 7)), reads=[uT_b, wqk_b[pb_]], writes=[pbuf[bk]])
                                    if which == 0:
                                        we_ = "act" if tg % 2 == 0 else "dve"
                                        evac(qTA[pb_][0:64, tg * 512:(tg + 1) * 512], pbank[bk][0:64, :], [pbuf[bk]], [qk_b[pb_]], scale=scale,
                                             which=we_)
                                        evac(qTB[pb_][64:128, tg * 512:(tg + 1) * 512], pbank[bk][64:128, :], [pbuf[bk]], [qk_b[pb_]], scale=scale,
                                             which=we_)
                                    else:
                                        evac(dst[:, tg * 512:(tg + 1) * 512], pbank[bk][:], [pbuf[bk]], [qk_b[pb_]], scale=scale)
                            wstash(pend_qk[hp], wqk[pb_][:].rearrange("p c t n -> p (c t n)"), wqk_b[pb_])
                            if hp + 1 < 4:
                                load_qk(hp + 1)
                            if s == 0 and PRECAST:
                                for j in range(hp * 6, min(NFF, hp * 6 + 6)):
                                    dst4 = wfi_bf[j].rearrange("p (k t n) -> p k t n", k=8, t=2)
                                    for t_ in range(2):
                                        cx.dma("pool", dst4[:, :, t_, :],
                                               w_fi_d[:, t_ * DFF + j * 128:t_ * DFF + (j + 1) * 128].rearrange("(k p) n -> p k n", p=128),
                                               writes=[scr_b[j]])
                            streams = [sb_stream(0, hp, 0, (0, 3)), sb_stream(2, hp, 0, (1, 2)),
                                       sb_stream(1, hp, 1, (0, 3)), sb_stream(3, hp, 1, (1, 2))]
                            live = list(streams)
                            if SEQ_STREAMS == 1:
                                for gen in streams:
                                    for _ in gen:
                                        pass
                                live = []
                            elif SEQ_STREAMS in (2, 3, 4, 5):
                                pairs = {2: ((0, 1), (2, 3)), 3: ((0, 2), (1, 3)), 4: ((0, 2), (1,), (3,)), 5: ((1, 3), (0,), (2,))}[SEQ_STREAMS]
                                for pr in pairs:
                                    lv = [streams[i_] for i_ in pr]
                                    while lv:
                                        nx = []
                                        for gen in lv:
                                            try:
                                                next(gen)
                                                nx.append(gen)
                                            except StopIteration:
                                                pass
                                        lv = nx
                                live = []
                            while live:
                                nxt = []
                                for gen in live:
                                    try:
                                        next(gen)
                                        nxt.append(gen)
                                    except StopIteration:
                                        pass
                                live = nxt
                        if DEBUG and s == 0:
                            cx.dma("sp", dbg_osb, osb[:], reads=[osb_b])
                        cx.barrier()

                    with contextlib.ExitStack() as es_dl:
                        acc = sb("acc%d" % s, [128, 4, S], F32, es_dl)
                        acc_b = Buf()
                        dbias = sb("dbias%d" % s, [128, 4, 256], F32, es_dl)
                        dbias_b = Buf()
                        wd = sb("wd%d" % s, [128, 8, 3, 256], BF16, es_dl)
                        wd_b = Buf()
                        qd = sb("qd%d" % s, [128, 2, S], BF16, es_dl)
                        kd = sb("kd%d" % s, [128, 2, S], BF16, es_dl)
                        qkd_b = Buf()
                        vd = sb("vd%d" % s, [128, NT, 4, 128], BF16, es_dl)
                        vd_b = Buf()
                        tb = [sb("tb%d_%d" % (s, i), [128, 256], F32, es_dl) for i in range(4)]
                        tb_b = [Buf() for _ in range(4)]
                        pbf = [sb("pbf%d_%d" % (s, i), [128, 256], BF16, es_dl) for i in range(4)]
                        pbf_b = [Buf() for _ in range(4)]
                        shf = sb("shf%d" % s, [128, 128], F32, es_dl)
                        den = sb("den%d" % s, [128, 512], F32, es_dl)
                        den_b = Buf()
                        rden = [sb("rden%d_%d" % (s, i), [128, 512], F32, es_dl) for i in range(2)]
                        rden_b = [Buf(), Buf()]
                        shf_b = Buf()
                        cx.dma("sp", shf[:], cmat_d[:, 3, :], writes=[shf_b])
                        for hh in range(4):
                            o1 = 64 if hh % 2 == 0 else 0
                            cx.op(dve, lambda: nc.vector.memset(vd[:, :, hh, o1:o1 + 64], 1.0), writes=[vd_b])
                        blk = [0]
                        nbat = [0]

                        pend_wd = {}
                        wd_flat = wd[:].rearrange("p c t n -> p (c t n)")

                        def load_wd(gi_):
                            pend_wd[gi_] = wload_c("wd%d" % gi_, s, wd_flat, [
                                (wd[:, :, j, :], w_in_d[:, base + 256 * gi_:base + 256 * (gi_ + 1)].rearrange("(c p) n -> p c n", p=128))
                                for j, base in enumerate((1536, 2304, 3072))], wd_b)

                        load_wd(0)
                        for gi in range(3):
                            d = DIL[gi]
                            nbr = NT // d
                            cx.dma("sp", dbias[:], dbias_d[:, 4 * gi:4 * gi + 4, :], writes=[dbias_b])
                            for which, dst, scale in ((0, qd, 0.125), (1, kd, None)):
                                for pp in range(2):
                                    for tg in range(4):
                                        bk = next_bank()
                                        for c in range(8):
                                            cx.op(pe, lambda c=c, bk=bk, tg=tg, which=which, pp=pp: nc.tensor.matmul(
                                                pbank[bk][:], lhsT=wd[:, c, which, pp * 128:(pp + 1) * 128],
                                                rhs=uT[:, c, tg * 512:(tg + 1) * 512],
                                                start=(c == 0), stop=(c == 7)), reads=[uT_b, wd_b], writes=[pbuf[bk]])
                                        evac(dst[:, pp, tg * 512:(tg + 1) * 512], pbank[bk][:], [pbuf[bk]], [qkd_b], scale=scale)

                            def sub(t3, r, m):
                                if d == 1:
                                    return t3[:, m * 128:(m + 1) * 128]
                                return t3.rearrange("p (i d) -> p d i", d=d)[:, r, m * 128:(m + 1) * 128]

                            for r in range(d):
                                for m in range(nbr):
                                    ti = r * nbr + m
                                    bk = next_bank()
                                    for c in range(8):
                                        cx.op(pe, lambda c=c, bk=bk, r=r, m=m: nc.tensor.matmul(
                                            pbank[bk][:, 0:256], lhsT=sub(uT[:, c, :], r, m), rhs=wd[:, c, 2, :],
                                            start=(c == 0), stop=(c == 7)), reads=[uT_b, wd_b], writes=[pbuf[bk]])
                                    we_ = "act" if ti % 2 == 0 else "dve"
                                    for par in range(2):
                                        evac(vd[:, ti, :, :].rearrange("p (a b) e -> p a b e", b=2)[:, :, par, par * 64:par * 64 + 64],
                                             pbank[bk][:, 0:256].rearrange("p (a b e) -> p a b e", a=2, b=2)[:, :, par, :],
                                             [pbuf[bk]], [vd_b], which=we_)
                            wstash(pend_wd[gi], wd_flat, wd_b)
                            if gi + 1 < 3:
                                load_wd(gi + 1)
                            for hh in range(4):
                                h = 4 * gi + hh
                                pp = hh // 2
                                R = slice((hh % 2) * 64, (hh % 2 + 1) * 64)
                                items = []
                                for r in range(d):
                                    for n0 in range(0, nbr, 4):
                                        nn = min(4, nbr - n0)
                                        bN = 5 + (nbat[0] % 2)
                                        nbat[0] += 1
                                        for n in range(n0, n0 + nn):
                                            items.append((r, n0, nn, n, bN))

                                def dl_scores(it, u):
                                    r, n0, nn, n, bN = it
                                    bS = u % 4
                                    wb = u % 4
                                    lo = 0 if n >= 1 else 128
                                    qv = sub(qd[R, pp, :], r, n)
                                    cx.op(pe, lambda: nc.tensor.matmul(
                                        pbank[bS][:, 128:256], lhsT=sub(kd[R, pp, :], r, n), rhs=qv,
                                        start=True, stop=True), reads=[qkd_b], writes=[pbuf[bS]])
                                    if n >= 1:
                                        cx.op(pe, lambda: nc.tensor.matmul(
                                            pbank[bS][:, 0:128], lhsT=sub(kd[R, pp, :], r, n - 1), rhs=qv,
                                            start=True, stop=True), reads=[qkd_b], writes=[pbuf[bS]])
                                    cx.op(dve, lambda: nc.vector.tensor_tensor(
                                        out=tb[wb][:, lo:256], in0=pbank[bS][:, lo:256], in1=dbias[:, hh, lo:256],
                                        op=ALU.add), reads=[pbuf[bS], dbias_b], writes=[tb_b[wb]])
                                    cx.op(act, lambda: nc.scalar.activation(out=pbf[wb][:, lo:256], in_=tb[wb][:, lo:256],
                                                                            func=AF.Exp),
                                          reads=[tb_b[wb]], writes=[pbf_b[wb]])

                                def dl_pv(it, u):
                                    r, n0, nn, n, bN = it
                                    wb = u % 4
                                    oc = (n - n0) * 128
                                    if n >= 1:
                                        cx.op(pe, lambda: nc.tensor.matmul(
                                            pbank[bN][:, oc:oc + 128], lhsT=vd[:, r * nbr + n - 1, hh, :],
                                            rhs=pbf[wb][:, 0:128], start=True, stop=False),
                                            reads=[pbf_b[wb], vd_b], writes=[pbuf[bN]])
                                    cx.op(pe, lambda: nc.tensor.matmul(
                                        pbank[bN][:, oc:oc + 128], lhsT=vd[:, r * nbr + n, hh, :],
                                        rhs=pbf[wb][:, 128:256], start=(n == 0), stop=True),
                                        reads=[pbf_b[wb], vd_b], writes=[pbuf[bN]])
                                    if n == n0 + nn - 1:
                                        if d == 1:
                                            av = acc[:, hh, n0 * 128:(n0 + nn) * 128]
                                        else:
                                            av = acc[:, hh, :].rearrange("p (i d) -> p d i", d=d)[:, r, n0 * 128:(n0 + nn) * 128]
                                        if gi == 0:
                                            cx.op(dve, lambda: nc.vector.tensor_copy(out=av, in_=pbank[bN][:, 0:nn * 128]),
                                                  reads=[pbuf[bN]], writes=[acc_b])
                                        else:
                                            cx.op(dve, lambda: nc.vector.tensor_tensor(out=av, in0=av, in1=pbank[bN][:, 0:nn * 128],
                                                                                       op=ALU.add),
                                                  reads=[pbuf[bN], acc_b], writes=[acc_b])

                                pend = []
                                for it in items:
                                    u = blk[0]
                                    blk[0] += 1
                                    dl_scores(it, u)
                                    pend.append((it, u))
                                    if len(pend) > 3:
                                        dl_pv(*pend.pop(0))
                                while pend:
                                    dl_pv(*pend.pop(0))
                        den2 = [den, rden[0]]
                        den2_b = [den_b, Buf()]
                        rd2 = sb("rden2_%d" % s, [128, 512], F32, es_dl)
                        rdn = [rden[1], rd2]
                        rdn_b = [rden_b[1], Buf()]

                        def nrm_ln(it):
                            hh, tg = divmod(it, 4)
                            PR = slice((hh % 2) * 64, (hh % 2 + 1) * 64)
                            bk = next_bank((0, 1, 2, 3))
                            cx.op(pe, lambda: nc.tensor.matmul(
                                pbank[bk][:, :], lhsT=shf[:, :], rhs=acc[:, hh, tg * 512:(tg + 1) * 512],
                                start=True, stop=True), reads=[acc_b, shf_b], writes=[pbuf[bk]])
                            cx.op(act, lambda: nc.scalar.activation(out=den2[it % 2][PR, :], in_=pbank[bk][PR, :], func=AF.Ln),
                                  reads=[pbuf[bk]], writes=[den2_b[it % 2]])

                        def nrm_exp(it):
                            hh, tg = divmod(it, 4)
                            PR = slice((hh % 2) * 64, (hh % 2 + 1) * 64)
                            di = it % 2
                            cx.op(act, lambda: nc.scalar.activation(out=rdn[di][PR, :], in_=den2[di][PR, :], func=AF.Exp, scale=-1.0),
                                  reads=[den2_b[di]], writes=[rdn_b[di]])
                            cx.op(dve, lambda: nc.vector.tensor_tensor(
                                out=odl[PR, hh // 2, tg * 512:(tg + 1) * 512], in0=acc[PR, hh, tg * 512:(tg + 1) * 512],
                                in1=rdn[di][PR, :], op=ALU.mult), reads=[rdn_b[di], acc_b], writes=[odl_b])

                        nrm_ln(0)
                        for it in range(16):
                            if it + 1 < 16:
                                nrm_ln(it + 1)
                            nrm_exp(it)
                        if DEBUG and s == 0:
                            cx.dma("sp", dbg_odl, odl[:], reads=[odl_b])
                            cx.dma("sp", dbg_acc, acc[:], reads=[acc_b])
                        cx.barrier()

                    with contextlib.ExitStack() as es_d1:
                        wg = [sb("wg%d_%d" % (s, i), [128, 8, 2, 128], BF16, es_d1) for i in range(2)]
                        wsu = [sb("wsu%d_%d" % (s, i), [128, 4, 128], BF16, es_d1) for i in range(2)]
                        wdu = [sb("wdu%d_%d" % (s, i), [128, 2, 128], BF16, es_d1) for i in range(2)]
                        w1_b = [Buf(), Buf()]
                        t1 = [sb("t1_%d_%d" % (s, i), [128, 512], F32, es_d1) for i in range(2)]
                        t2 = [sb("t2_%d_%d" % (s, i), [128, 512], F32, es_d1) for i in range(2)]
                        t_b = [Buf(), Buf()]
                        m1 = [sb("m1_%d_%d" % (s, i), [128, 512], F32, es_d1) for i in range(2)]
                        m2 = [sb("m2_%d_%d" % (s, i), [128, 512], F32, es_d1) for i in range(2)]
                        m_b = [Buf(), Buf()]
                        it = 0

                        pend_d1 = {}

                        def d1_flats(wi):
                            return (wg[wi][:].rearrange("p k t n -> p (k t n)"), wsu[wi][:].rearrange("p h n -> p (h n)"),
                                    wdu[wi][:].rearrange("p h n -> p (h n)"))

                        def load_d1(c):
                            wi = c % 2
                            f1, f2, f3 = d1_flats(wi)
                            pend_d1[c] = (
                                wload_c("wg%d" % c, s, f1, [
                                    (wg[wi][:, :, 0, :], w_in_d[:, 3840 + c * 128:3840 + (c + 1) * 128].rearrange("(k p) n -> p k n", p=128)),
                                    (wg[wi][:, :, 1, :], w_in_d[:, 4864 + c * 128:4864 + (c + 1) * 128].rearrange("(k p) n -> p k n", p=128))],
                                    w1_b[wi]),
                                wload_c("wsu%d" % c, s, f2, [(wsu[wi][:], w_sbu_d[:, c * 128:(c + 1) * 128].rearrange("(h p) n -> p h n", p=128))],
                                        w1_b[wi]),
                                wload_c("wdu%d" % c, s, f3, [(wdu[wi][:], w_dlu_d[:, c * 128:(c + 1) * 128].rearrange("(h p) n -> p h n", p=128))],
                                        w1_b[wi]))

                        load_d1(0)
                        for c in range(8):
                            wi = c % 2
                            if c + 1 < 8:
                                load_d1(c + 1)
                            stash_d1 = True
                            for tg in range(4):
                                ts_ = slice(tg * 512, (tg + 1) * 512)
                                b1, b2, b3, b4 = ((0, 1, 2, 3) if it % 2 == 0 else (4, 5, 6, 0))
                                if it % 2 == 1:
                                    b1, b2, b3, b4 = 4, 5, 6, 3
                                wk = it % 2
                                it += 1
                                for h in range(4):
                                    cx.op(pe, lambda h=h: nc.tensor.matmul(pbank[b1][:], lhsT=wsu[wi][:, h, :], rhs=osb[:, h, ts_],
                                                                           start=(h == 0), stop=(h == 3)),
                                          reads=[osb_b, w1_b[wi]], writes=[pbuf[b1]])
                                for h in range(2):
                                    cx.op(pe, lambda h=h: nc.tensor.matmul(pbank[b2][:], lhsT=wdu[wi][:, h, :], rhs=odl[:, h, ts_],
                                                                           start=(h == 0), stop=(h == 1)),
                                          reads=[odl_b, w1_b[wi]], writes=[pbuf[b2]])
                                for k in range(8):
                                    cx.op(pe, lambda k=k: nc.tensor.matmul(pbank[b3][:], lhsT=wg[wi][:, k, 0, :], rhs=uT[:, k, ts_],
                                                                           start=(k == 0), stop=(k == 7)),
                                          reads=[uT_b, w1_b[wi]], writes=[pbuf[b3]])
                                for k in range(8):
                                    cx.op(pe, lambda k=k: nc.tensor.matmul(pbank[b4][:], lhsT=wg[wi][:, k, 1, :], rhs=uT[:, k, ts_],
                                                                           start=(k == 0), stop=(k == 7)),
                                          reads=[uT_b, w1_b[wi]], writes=[pbuf[b4]])
                                cx.op(act, lambda: nc.scalar.activation(out=t1[wk][:], in_=pbank[b3][:], func=AF.Tanh, scale=0.5),
                                      reads=[pbuf[b3]], writes=[t_b[wk]])
                                cx.op(act, lambda: nc.scalar.activation(out=t2[wk][:], in_=pbank[b4][:], func=AF.Tanh, scale=0.5),
                                      reads=[pbuf[b4]], writes=[t_b[wk]])
                                cx.op(dve, lambda: nc.vector.scalar_tensor_tensor(out=m1[wk][:], in0=t1[wk][:], scalar=1.0,
                                                                                  in1=pbank[b1][:], op0=ALU.add, op1=ALU.mult),
                                      reads=[t_b[wk], pbuf[b1]], writes=[m_b[wk]])
                                cx.op(dve, lambda: nc.vector.scalar_tensor_tensor(out=m2[wk][:], in0=t2[wk][:], scalar=1.0,
                                                                                  in1=pbank[b2][:], op0=ALU.add, op1=ALU.mult),
                                      reads=[t_b[wk], pbuf[b2]], writes=[m_b[wk]])
                                cx.op(dve, lambda: nc.vector.tensor_tensor(out=mT[:, c, ts_], in0=m1[wk][:], in1=m2[wk][:], op=ALU.add),
                                      reads=[m_b[wk]], writes=[mT_b])
                                if stash_d1:
                                    stash_d1 = False
                                    for pnd, fl in zip(pend_d1[c], d1_flats(wi)):
                                        wstash(pnd, fl, w1_b[wi])
                        if DEBUG and s == 0:
                            cx.dma("sp", dbg_mT, mT[:], reads=[mT_b])
                        cx.barrier()

                with contextlib.ExitStack() as es_d:
                    g2 = sb("g2_%d" % s, [128, D], F32, es_d)
                    gf = sb("gf_%d" % s, [128, D], F32, es_d)
                    cx.dma("sp", g2[:], g2_d, writes=[g_b])
                    cx.dma("sp", gf[:], gf_d, writes=[g_b])
                    wfo = sb("wfo%d" % s, [128, NFF, D], BF16, es_d)
                    x1 = [[sb("x1_%d_%d_%d" % (s, p_, i), [128, D], F32, es_d) for i in range(4)] for p_ in range(2)]
                    x1_b = [[Buf() for _ in range(4)] for _ in range(2)]
                    u2T = [sb("u2T%d_%d" % (s, p_), [128, 8, 512], BF16, es_d) for p_ in range(2)]
                    u2T_b = [Buf(), Buf()]
                    hT = sb("hT%d" % s, [128, NFF, 512], BF16, es_d)
                    hT_b = [Buf() for _ in range(NFF)]
                    wf = [sb("wf%d_%d" % (s, i), [128, 8, 2, 128], BF16, es_d) for i in range(NWF)]
                    wf_b = [Buf() for _ in range(NWF)]
                    ub = [sb("ubd%d_%d" % (s, i), [128, D], BF16, es_d) for i in range(4)]
                    ub_b = [Buf() for _ in range(4)]
                    sg = [sb("sg%d_%d" % (s, i), [128, 512], F32, es_d) for i in range(2)]
                    sg_b = [Buf(), Buf()]
                    st2_b = Buf()
                    st3_b = Buf()
                    wfo_bs = [Buf() for _ in range(NFF // 2)]

                    def load_wfo(pi):
                        j0 = 2 * pi
                        fl = wfo[:, j0:j0 + 2, :].rearrange("p k n -> p (k n)")
                        pnd = wload_c("wfo%d" % pi, s, fl, [
                            (wfo[:, j0:j0 + 2, :], w_fo_d[j0 * 128:(j0 + 2) * 128, :].rearrange("(k p) n -> p k n", p=128))], wfo_bs[pi])
                        return pnd, fl
                    def load_wf(idx):
                        j = idx % NFF
                        wi = idx % NWF
                        flat = wf[wi][:].rearrange("p k t n -> p (k t n)")
                        if s == 0 and idx < NFF and not PRECAST:
                            wload(wf[wi][:, :, 0, :], w_fi_d[:, j * 128:(j + 1) * 128].rearrange("(k p) n -> p k n", p=128), wf_b[wi])
                            wload(wf[wi][:, :, 1, :],
                                  w_fi_d[:, DFF + j * 128:DFF + (j + 1) * 128].rearrange("(k p) n -> p k n", p=128), wf_b[wi])
                            cx.dma("sp", wfi_bf[j], flat, reads=[wf_b[wi]], writes=[scr_b[j]])
                        else:
                            cx.dma("sp", flat, wfi_bf[j], reads=[scr_b[j]], writes=[wf_b[wi]])

                    for i_ in range(PFD):
                        load_wf(i_)
                    wfo_pend = []
                    def d2_front(tq):
                        par = tq % 2
                        for i in range(4):
                            tok0 = tq * 512 + i * 128
                            cx.dma("sp", x1[par][i][:], x_d[s, tok0:tok0 + 128, :], writes=[x1_b[par][i]])
                            for half in range(2):
                                bk = next_bank()
                                for k in range(8):
                                    cx.op(pe, lambda k=k: nc.tensor.matmul(
                                        pbank[bk][:], lhsT=mT[:, k, tok0:tok0 + 128], rhs=wo[:, k, half * 512:(half + 1) * 512],
                                        start=(k == 0), stop=(k == 7)), reads=[mT_b, wo_b], writes=[pbuf[bk]])
                                hs_ = slice(half * 512, (half + 1) * 512)
                                cx.op(dve, lambda: nc.vector.scalar_tensor_tensor(
                                    out=x1[par][i][:, hs_], in0=pbank[bk][:], scalar=0.5,
                                    in1=x1[par][i][:, hs_], op0=ALU.mult, op1=ALU.add),
                                    reads=[pbuf[bk], x1_b[par][i]], writes=[x1_b[par][i]])
                            rms_stats(x1[par][i][:], x1_b[par][i], 32 + i, st2_b)
                        rstd_batch(32, 4, 36, st2_b)
                        if DEBUG and s == 0 and tq == 0:
                            for i in range(4):
                                cx.dma("sp", dbg_x1[i], x1[par][i][:], reads=[x1_b[par][i]])

                    def d2_scale(tq):
                        par = tq % 2
                        for i in range(4):
                            cx.op(dve, lambda: nc.vector.scalar_tensor_tensor(out=ub[i][:], in0=x1[par][i][:], scalar=stat[:, 36 + i:37 + i],
                                                                              in1=g2[:], op0=ALU.mult, op1=ALU.mult),
                                  reads=[x1_b[par][i], st2_b, g_b], writes=[ub_b[i]])

                    def d2_tr(tq):
                        par = tq % 2
                        for i in range(4):
                            pt_, ptb_ = ptrs[ptr_rr[0] % 2]
                            ptr_rr[0] += 1
                            for c in range(8):
                                cx.op(pe, lambda c=c: nc.tensor.transpose(pt_[:, c * 128:(c + 1) * 128], ub[i][:, c * 128:(c + 1) * 128],
                                                                          ident),
                                      reads=[ub_b[i], cmat_b], writes=[ptb_])
                            evac(u2T[par][:, :, i * 128:(i + 1) * 128], pt_[:].rearrange("p (c n) -> p c n", c=8), [ptb_], [u2T_b[par]])

                    def d5(tq):
                        par = tq % 2
                        for j in range(NFF):
                            idx = tq * NFF + j
                            wi = idx % NWF
                            if idx + PFD < 4 * NFF:
                                load_wf(idx + PFD)
                            if tq == 0 and j % 2 == 0:
                                pnd_, fl_ = load_wfo(j // 2)
                                if pnd_ is not None:
                                    wfo_pend.append((pnd_, fl_, wfo_bs[j // 2]))
                            if tq == 0 and j % 2 == 1 and wfo_pend and len(wfo_pend) > 2:
                                p0_, f0_, b0_ = wfo_pend.pop(0)
                                wstash(p0_, f0_, b0_)
                            bg = next_bank()
                            bu = next_bank()
                            for k in range(8):
                                cx.op(pe, lambda k=k: nc.tensor.matmul(pbank[bg][:], lhsT=wf[wi][:, k, 0, :], rhs=u2T[par][:, k, :],
                                                                       start=(k == 0), stop=(k == 7)),
                                      reads=[u2T_b[par], wf_b[wi]], writes=[pbuf[bg]])
                            for k in range(8):
                                cx.op(pe, lambda k=k: nc.tensor.matmul(pbank[bu][:], lhsT=wf[wi][:, k, 1, :], rhs=u2T[par][:, k, :],
                                                                       start=(k == 0), stop=(k == 7)),
                                      reads=[u2T_b[par], wf_b[wi]], writes=[pbuf[bu]])
                            cx.op(act, lambda: nc.scalar.activation(out=sg[j % 2][:], in_=pbank[bg][:], func=AF.Silu),
                                  reads=[pbuf[bg]], writes=[sg_b[j % 2]])
                            cx.op(dve, lambda: nc.vector.tensor_tensor(out=hT[:, j, :], in0=sg[j % 2][:], in1=pbank[bu][:], op=ALU.mult),
                                  reads=[sg_b[j % 2], pbuf[bu]], writes=[hT_b[j]])

                    def d6(tq):
                        par = tq % 2
                        for i in range(4):
                            for half in range(2):
                                bk = next_bank()
                                for j in range(NFF):
                                    cx.op(pe, lambda j=j: nc.tensor.matmul(
                                        pbank[bk][:], lhsT=hT[:, j, i * 128:(i + 1) * 128], rhs=wfo[:, j, half * 512:(half + 1) * 512],
                                        start=(j == 0), stop=(j == NFF - 1)), reads=[hT_b[j], wfo_bs[j // 2]], writes=[pbuf[bk]])
                                hs_ = slice(half * 512, (half + 1) * 512)
                                cx.op(dve, lambda: nc.vector.tensor_tensor(
                                    out=x1[par][i][:, hs_], in0=pbank[bk][:], in1=x1[par][i][:, hs_], op=ALU.add),
                                    reads=[pbuf[bk], x1_b[par][i]], writes=[x1_b[par][i]])
                            rms_stats(x1[par][i][:], x1_b[par][i], 40 + i, st3_b)
                        while wfo_pend:
                            p0_, f0_, b0_ = wfo_pend.pop(0)
                            wstash(p0_, f0_, b0_)

                    def d7(tq):
                        par = tq % 2
                        rstd_batch(40, 4, 44, st3_b)
                        for i in range(4):
                            tok0 = tq * 512 + i * 128
                            cx.op(dve, lambda: nc.vector.scalar_tensor_tensor(out=x1[par][i][:], in0=x1[par][i][:],
                                                                              scalar=stat[:, 44 + i:45 + i],
                                                                              in1=gf[:], op0=ALU.mult, op1=ALU.mult),
                                  reads=[x1_b[par][i], st3_b, g_b], writes=[x1_b[par][i]])
                            cx.dma("sp", out_d[s, tok0:tok0 + 128, :], x1[par][i][:], reads=[x1_b[par][i]])

                    d2_front(0)
                    d2_scale(0)
                    d2_tr(0)
                    for tq in range(4):
                        d5(tq)
                        if tq + 1 < 4:
                            d2_front(tq + 1)
                            d2_scale(tq + 1)
                        d6(tq)
                        if tq + 1 < 4:
                            d2_tr(tq + 1)
                        d7(tq)
                    cx.barrier()
    nc._cx = cx
    return nc


def _consts():
    cm = np.zeros((128, 4, 128), np.float32)
    cm[:, 0, :] = np.eye(128, dtype=np.float32)
    j = np.arange(128)[:, None]
    s_ = np.arange(128)[None, :]
    cm[:, 1, :] = np.where(j >= s_, -1.0, 0.0)
    cm[:, 2, :] = np.where(j < s_, 0.0, NEG)
    cm[:, 3, :] = (np.arange(128)[:, None] == ((np.arange(128) + 64) % 128)[None, :]).astype(np.float32)
    slopes = np.exp2(-8.0 * np.arange(1, 13, dtype=np.float64) / 12.0)
    db = np.full((128, 12, 256), NEG, np.float32)
    kc = np.arange(128)[:, None]
    qa = np.arange(128)[None, :]
    for h in range(12):
        cdil = slopes[h] * DIL[h // 4]
        prev = np.where(kc >= qa, -cdil * (128 + qa - kc), NEG)
        cur = np.where(kc <= qa, -cdil * (qa - kc), NEG)
        db[:, h, 0:128] = prev
        db[:, h, 128:256] = cur
    return cm, db


_CACHE = {}


def kernel(x, norm_mix_g, w_in, w_sb_up, w_dil_up, w_out, norm_ffn_g, w_ffn_in, w_ffn_out, norm_final_g):
    f = lambda a: np.ascontiguousarray(np.asarray(a, dtype=np.float32))
    x = f(x)
    n = 8
    if "nc" not in _CACHE:
        _CACHE["nc"] = build_program()
    nc = _CACHE["nc"]
    cm, db = _consts()
    bc = lambda g: np.ascontiguousarray(np.broadcast_to(f(g).reshape(1, D), (128, D)))
    shared = {
        "g1": bc(norm_mix_g), "g2": bc(norm_ffn_g), "gf": bc(norm_final_g),
        "w_in": f(w_in).reshape(D, INW), "w_sb_up": f(w_sb_up).reshape(512, D),
        "w_dil_up": f(w_dil_up).reshape(256, D), "w_out": f(w_out).reshape(D, D),
        "w_ffn_in": f(w_ffn_in).reshape(D, 2 * DFF), "w_ffn_out": f(w_ffn_out).reshape(DFF, D),
        "cmat": cm, "dbias": db,
    }
    in_maps = []
    for i in range(n):
        m = dict(shared)
        m["x"] = np.ascontiguousarray(x[NSEQ * i:NSEQ * (i + 1)])
        in_maps.append(m)
    res = run_bass_kernel_spmd(nc, in_maps, core_ids=list(range(n)))
    return np.concatenate([r["out"] for r in res.results], axis=0).astype(np.float32)
```
